# Optimizing a Trainium2 kernel written in Bass

```python
import math
import jax, jax.numpy as jnp
from jax import lax
import numpy as np


D_MODEL = 1024
BATCH = 8
SEQ = 8192
DEPTH = 1

PLE_DIM = 256
NSA_HEADS = 8
NSA_KV_GROUPS = 2
NSA_REP = NSA_HEADS // NSA_KV_GROUPS
NSA_DH = 64
NSA_WIDTH = NSA_HEADS * NSA_DH
NSA_KV = NSA_KV_GROUPS * NSA_DH
CMP_LEN = 32
CMP_STRIDE = 16
CMP_HIDDEN = 256
SEL_BLOCK = 64
TOP_N = 16
N_LOCAL = 2
WINDOW = 512
Q_BLOCK = 128
GLA_HEADS = 4
GLA_DK = 64
GLA_DV = 128
GLA_WIDTH = GLA_HEADS * GLA_DV
GLA_RANK = 16
GLA_TAU = 16.0
GLA_CHUNK = 64
MIX_WIDTH = NSA_WIDTH + GLA_WIDTH
NUM_BUCKETS = 32
MAX_DISTANCE = 128
ALPHA = (2.0 * DEPTH) ** 0.25
BETA = (8.0 * DEPTH) ** -0.25
EPS = 1e-5
NEG_INF = -1e30
POS_INF = 1e30
IN_WIDTHS = (NSA_WIDTH, NSA_KV, NSA_KV, NSA_KV, NSA_KV, NSA_KV, NSA_KV, 3 * NSA_HEADS, NSA_WIDTH,
             GLA_HEADS * GLA_DK, GLA_HEADS * GLA_DK, GLA_WIDTH, GLA_RANK, GLA_WIDTH)
IN_SPLIT_POINTS = tuple(int(v) for v in np.cumsum(IN_WIDTHS)[:-1])
D_IN = sum(IN_WIDTHS)

kernel_name = 'hymba_nsa_gla_deepnorm'


def layer_norm(x, g, b):
    xf = x.astype(jnp.float32)
    mu = jnp.mean(xf, axis=-1, keepdims=True)
    var = jnp.mean(jnp.square(xf - mu), axis=-1, keepdims=True)
    return (xf - mu) * lax.rsqrt(var + EPS) * g + b


def rms_norm(x, g):
    xf = x.astype(jnp.float32)
    return xf * lax.rsqrt(jnp.mean(jnp.square(xf), axis=-1, keepdims=True) + EPS) * g


def t5_bucket(dist):
    n = jnp.maximum(dist, 0)
    max_exact = NUM_BUCKETS // 2
    nf = jnp.maximum(n, 1).astype(jnp.float32)
    large = max_exact + (jnp.log(nf / max_exact) / math.log(MAX_DISTANCE / max_exact)
                         * (NUM_BUCKETS - max_exact)).astype(jnp.int32)
    large = jnp.minimum(large, NUM_BUCKETS - 1)
    return jnp.where(n < max_exact, n, large)


def compress_blocks(kv, pos, w1, b1, w2):
    B, S, G, DH = kv.shape
    ch = kv.transpose(0, 2, 1, 3).reshape(B, G, S // CMP_STRIDE, CMP_STRIDE, DH)
    blocks = jnp.concatenate([ch[:, :, :-1], ch[:, :, 1:]], axis=3) + pos
    nc = blocks.shape[2]
    h = jax.nn.gelu(blocks.reshape(B, G, nc, CMP_LEN * DH) @ w1 + b1)
    return h @ w2


def nsa_attention(q, kc, vc, ks, vs, kw, vw, gates, rel_bias):
    B, S = q.shape[0], q.shape[1]
    G, R, DH = NSA_KV_GROUPS, NSA_REP, NSA_DH
    qg = q.reshape(B, S, G, R, DH).transpose(0, 2, 3, 1, 4)
    nc = kc.shape[2]
    nsel = S // SEL_BLOCK
    top_n = min(TOP_N, nsel)
    ks_blk = ks.transpose(0, 2, 1, 3).reshape(B, G, nsel, SEL_BLOCK, DH)
    vs_blk = vs.transpose(0, 2, 1, 3).reshape(B, G, nsel, SEL_BLOCK, DH)
    pad = ((0, 0), (0, 0), (WINDOW, 0), (0, 0))
    kw_pad = jnp.pad(kw.transpose(0, 2, 1, 3), pad)
    vw_pad = jnp.pad(vw.transpose(0, 2, 1, 3), pad)
    rb = rel_bias.reshape(NUM_BUCKETS, G, R)
    scale = DH ** -0.5
    cmp_end = jnp.arange(nc) * CMP_STRIDE + CMP_LEN - 1
    c_start = jnp.arange(nc) * CMP_STRIDE
    b_start = jnp.arange(nsel) * SEL_BLOCK
    overlap = jnp.clip(jnp.minimum(c_start[:, None] + CMP_LEN, b_start[None, :] + SEL_BLOCK)
                       - jnp.maximum(c_start[:, None], b_start[None, :]), 0, None).astype(jnp.float32) / CMP_LEN
    c_win = jnp.arange(Q_BLOCK + WINDOW)
    dist_w = jnp.arange(Q_BLOCK)[:, None] + WINDOW - c_win[None, :]
    band = (dist_w >= 0) & (dist_w < WINDOW)
    bias_w = rb[t5_bucket(dist_w)].transpose(2, 3, 0, 1)
    blk = jnp.arange(nsel)
    g_idx = jnp.arange(G)[None, :, None, None]
    gather = jax.vmap(jax.vmap(lambda blocks, ix: blocks[ix]))

    def block(i):
        t0 = i * Q_BLOCK
        t = t0 + jnp.arange(Q_BLOCK)
        qb = lax.dynamic_slice_in_dim(qg, t0, Q_BLOCK, axis=3)
        valid_c = cmp_end[None, :] <= t[:, None]
        bias_c = rb[t5_bucket(t[:, None] - cmp_end[None, :])].transpose(2, 3, 0, 1)
        s_c = jnp.einsum('bgrqd,bgcd->bgrqc', qb, kc).astype(jnp.float32) * scale + bias_c
        p_c = jnp.where(valid_c, jax.nn.softmax(jnp.where(valid_c, s_c, NEG_INF), axis=-1), 0.0)
        o_c = jnp.einsum('bgrqc,bgcd->bgrqd', p_c, vc)
        imp = jnp.einsum('bgrqc,cn->bgqn', p_c, overlap)
        cur = (t // SEL_BLOCK)[:, None]
        causal_b = blk[None, :] <= cur
        forced = causal_b & ((blk[None, :] == 0) | (blk[None, :] >= cur - (N_LOCAL - 1)))
        imp = jnp.where(forced, POS_INF, jnp.where(causal_b, imp, NEG_INF))
        _, idx = lax.top_k(imp, top_n)
        k_sel = gather(ks_blk, idx).reshape(B, G, Q_BLOCK, top_n * SEL_BLOCK, DH)
        v_sel = gather(vs_blk, idx).reshape(B, G, Q_BLOCK, top_n * SEL_BLOCK, DH)
        pos = (idx[..., None] * SEL_BLOCK + jnp.arange(SEL_BLOCK)).reshape(B, G, Q_BLOCK, top_n * SEL_BLOCK)
        dist_s = t[:, None] - pos
        bias_s = jnp.moveaxis(rb[t5_bucket(dist_s), g_idx], -1, 2)
        s_s = jnp.einsum('bgrqd,bgqmd->bgrqm', qb, k_sel).astype(jnp.float32) * scale + bias_s
        p_s = jax.nn.softmax(jnp.where(dist_s[:, :, None] >= 0, s_s, NEG_INF), axis=-1)
        o_s = jnp.einsum('bgrqm,bgqmd->bgrqd', p_s, v_sel)
        kwb = lax.dynamic_slice_in_dim(kw_pad, t0, Q_BLOCK + WINDOW, axis=2)
        vwb = lax.dynamic_slice_in_dim(vw_pad, t0, Q_BLOCK + WINDOW, axis=2)
        valid_w = band & ((t0 - WINDOW + c_win) >= 0)[None, :]
        s_w = jnp.einsum('bgrqd,bgkd->bgrqk', qb, kwb).astype(jnp.float32) * scale + bias_w
        p_w = jax.nn.softmax(jnp.where(valid_w, s_w, NEG_INF), axis=-1)
        o_w = jnp.einsum('bgrqk,bgkd->bgrqd', p_w, vwb)
        gb = lax.dynamic_slice_in_dim(gates, t0, Q_BLOCK, axis=1)
        gb = gb.reshape(B, Q_BLOCK, G, R, 3).transpose(0, 2, 3, 1, 4)
        return gb[..., 0:1] * o_c + gb[..., 1:2] * o_s + gb[..., 2:3] * o_w

    out = lax.map(block, jnp.arange(S // Q_BLOCK))
    return out.transpose(1, 0, 4, 2, 3, 5).reshape(B, S, G * R * DH)


def gla_chunked(q, k, v, log_a):
    B, S, H, DK = q.shape
    DV = v.shape[-1]
    C = GLA_CHUNK
    N = S // C

    def to_chunks(a):
        return a.astype(jnp.float32).reshape(B, N, C, H, a.shape[-1]).transpose(0, 3, 1, 2, 4)

    q, k, v, log_a = to_chunks(q), to_chunks(k), to_chunks(v), to_chunks(log_a)
    b = jnp.cumsum(log_a, axis=3)
    b_last = b[:, :, :, -1:, :]
    q_e = q * (DK ** -0.5) * jnp.exp(b)
    k_e = k * jnp.exp(-b)
    k_last = k * jnp.exp(b_last - b)
    causal = jnp.tril(jnp.ones((C, C), dtype=bool))
    attn = jnp.where(causal, jnp.einsum('bhncd,bhnsd->bhncs', q_e, k_e), 0.0)
    o_intra = jnp.einsum('bhncs,bhnsv->bhncv', attn, v)
    kv = jnp.einsum('bhnsd,bhnsv->bhndv', k_last, v)
    decay = jnp.exp(b_last[:, :, :, 0, :])

    def step(state, inp):
        q_n, kv_n, d_n = inp
        o = jnp.einsum('bhcd,bhdv->bhcv', q_n, state)
        return d_n[..., None] * state + kv_n, o

    state0 = jnp.zeros((B, H, DK, DV), jnp.float32)
    _, o_inter = lax.scan(step, state0, (jnp.moveaxis(q_e, 2, 0), jnp.moveaxis(kv, 2, 0), jnp.moveaxis(decay, 2, 0)))
    o = o_intra + jnp.moveaxis(o_inter, 0, 2)
    return o.transpose(0, 2, 3, 1, 4).reshape(B, S, H, DV)


def hybrid_layer(x, p_l, rel_bias, w_in, w_a2, b_a, gla_norm_w, pos_cmp, w_ck1, b_ck1, w_ck2,
                 w_cv1, b_cv1, w_cv2, w_out, w_pe, w_pg, b_pg, ln_g, ln_b):
    B, S, _ = x.shape
    u = x @ w_in
    (q_n, kc_r, vc_r, ks_r, vs_r, kw_r, vw_r, g_n, z_n,
     q_g, k_g, v_g, a_low, z_g) = jnp.split(u, IN_SPLIT_POINTS, axis=-1)
    kvs = lambda a: a.reshape(B, S, NSA_KV_GROUPS, NSA_DH)
    kc = compress_blocks(kvs(kc_r), pos_cmp, w_ck1, b_ck1, w_ck2)
    vc = compress_blocks(kvs(vc_r), pos_cmp, w_cv1, b_cv1, w_cv2)
    gates = jax.nn.sigmoid(g_n.reshape(B, S, NSA_HEADS, 3))
    o_nsa = nsa_attention(q_n.reshape(B, S, NSA_HEADS, NSA_DH), kc, vc, kvs(ks_r), kvs(vs_r),
                          kvs(kw_r), kvs(vw_r), gates, rel_bias)
    o_nsa = o_nsa * jax.nn.silu(z_n)
    log_a = jax.nn.log_sigmoid((a_low @ w_a2 + b_a).astype(jnp.float32)) / GLA_TAU
    o_gla = gla_chunked(q_g.reshape(B, S, GLA_HEADS, GLA_DK), k_g.reshape(B, S, GLA_HEADS, GLA_DK),
                        v_g.reshape(B, S, GLA_HEADS, GLA_DV), log_a.reshape(B, S, GLA_HEADS, GLA_DK))
    o_gla = rms_norm(o_gla, gla_norm_w).reshape(B, S, GLA_WIDTH) * jax.nn.silu(z_g)
    y = jnp.concatenate([o_nsa, o_gla], axis=-1) @ w_out
    r = ALPHA * x + y
    r = r + jax.nn.sigmoid(r @ w_pg + b_pg) * (p_l @ w_pe)
    return layer_norm(r, ln_g, ln_b)


def setup_inputs(seed: int = 0) -> dict:
    key = jax.random.key(seed)
    ks = jax.random.split(key, 22)

    def nrm(k, shape, scale):
        return jax.random.normal(k, shape, jnp.float32) * scale

    return {
        'x': nrm(ks[0], (BATCH, SEQ, D_MODEL), 1.0),
        'p': nrm(ks[1], (DEPTH, BATCH, SEQ, PLE_DIM), 1.0),
        'ln0_g': 1.0 + nrm(ks[2], (D_MODEL,), 0.05),
        'ln0_b': nrm(ks[3], (D_MODEL,), 0.02),
        'rel_bias': nrm(ks[4], (NUM_BUCKETS, NSA_HEADS), 0.3),
        'w_in': nrm(ks[5], (DEPTH, D_MODEL, D_IN), D_MODEL ** -0.5),
        'w_a2': nrm(ks[6], (DEPTH, GLA_RANK, GLA_HEADS * GLA_DK), GLA_RANK ** -0.5),
        'b_a': nrm(ks[7], (DEPTH, GLA_HEADS * GLA_DK), 0.1),
        'gla_norm_w': 1.0 + nrm(ks[8], (DEPTH, GLA_DV), 0.05),
        'pos_cmp': nrm(ks[9], (DEPTH, CMP_LEN, NSA_DH), 0.1),
        'w_ck1': nrm(ks[10], (DEPTH, CMP_LEN * NSA_DH, CMP_HIDDEN), (CMP_LEN * NSA_DH) ** -0.5),
        'b_ck1': nrm(ks[11], (DEPTH, CMP_HIDDEN), 0.02),
        'w_ck2': nrm(ks[12], (DEPTH, CMP_HIDDEN, NSA_DH), CMP_HIDDEN ** -0.5),
        'w_cv1': nrm(ks[13], (DEPTH, CMP_LEN * NSA_DH, CMP_HIDDEN), (CMP_LEN * NSA_DH) ** -0.5),
        'b_cv1': nrm(ks[14], (DEPTH, CMP_HIDDEN), 0.02),
        'w_cv2': nrm(ks[15], (DEPTH, CMP_HIDDEN, NSA_DH), CMP_HIDDEN ** -0.5),
        'w_out': nrm(ks[16], (DEPTH, MIX_WIDTH, D_MODEL), MIX_WIDTH ** -0.5 * BETA),
        'w_pe': nrm(ks[17], (DEPTH, PLE_DIM, D_MODEL), PLE_DIM ** -0.5 * BETA),
        'w_pg': nrm(ks[18], (DEPTH, D_MODEL, D_MODEL), D_MODEL ** -0.5),
        'b_pg': nrm(ks[19], (DEPTH, D_MODEL), 0.02),
        'ln_g': 1.0 + nrm(ks[20], (DEPTH, D_MODEL), 0.05),
        'ln_b': nrm(ks[21], (DEPTH, D_MODEL), 0.02),
    }


def reference(x, p, ln0_g, ln0_b, rel_bias, w_in, w_a2, b_a, gla_norm_w, pos_cmp, w_ck1, b_ck1, w_ck2,
              w_cv1, b_cv1, w_cv2, w_out, w_pe, w_pg, b_pg, ln_g, ln_b):
    h = layer_norm(x, ln0_g, ln0_b)
    for i in range(DEPTH):
        h = hybrid_layer(h, p[i], rel_bias, w_in[i], w_a2[i], b_a[i], gla_norm_w[i], pos_cmp[i],
                         w_ck1[i], b_ck1[i], w_ck2[i], w_cv1[i], b_cv1[i], w_cv2[i], w_out[i],
                         w_pe[i], w_pg[i], b_pg[i], ln_g[i], ln_b[i])
    return h
```

```python
import math
from contextlib import ExitStack

import numpy as np
import concourse.bass as bass
import concourse.mybir as mybir
from concourse.bass_utils import run_bass_kernel_spmd

F32 = mybir.dt.float32
BF16 = mybir.dt.bfloat16
AF = mybir.ActivationFunctionType
ALU = mybir.AluOpType
AX = mybir.AxisListType

D = 1024
DIN = 3368
NEG = -30000.0
EPS = 1e-5
ALPHA = 2.0 ** 0.25


class Sched:
    COMPUTE = ("pe", "act", "dve", "pool")

    def __init__(self, nc, es):
        self.nc = nc
        self.es = es
        self.eng = {"pe": nc.tensor, "act": nc.scalar, "dve": nc.vector,
                    "pool": nc.gpsimd, "sp": nc.sync}
        self.sem = {}
        self.cnt = {}
        for e in self.COMPUTE:
            self.sem[e] = es.enter_context(nc.semaphore("s_" + e))
            self.cnt[e] = 0
        self.known = {e: {} for e in self.eng}
        self.lastw = {}
        self.readers = {}
        self.slots = {}
        self.semobj = {}

    def _slot(self, name):
        if name not in self.slots:
            s = self.es.enter_context(self.nc.semaphore("d_" + name))
            self.slots[name] = [s, 0]
        return self.slots[name]

    def _deps(self, reads, writes):
        d = {}
        for k in reads:
            for s, (v, e) in self.lastw.get(k, {}).items():
                if s not in d or d[s][0] < v:
                    d[s] = (v, e)
        for k in writes:
            for src in (self.lastw.get(k, {}), self.readers.get(k, {})):
                for s, (v, e) in src.items():
                    if s not in d or d[s][0] < v:
                        d[s] = (v, e)
        return d

    def _wait(self, e, deps, is_dma):
        for s, (v, src) in deps.items():
            if (not is_dma) and src == e and e == "pe":
                continue
            if self.known[e].get(s, 0) >= v:
                continue
            self.eng[e].wait_ge(self.semobj[s], v)
            self.known[e][s] = v

    def _record(self, t, reads, writes, multi=False):
        s, v, e = t
        for k in writes:
            if multi:
                self.lastw.setdefault(k, {})[s] = (v, e)
            else:
                self.lastw[k] = {s: (v, e)}
                self.readers[k] = {}
        for k in reads:
            self.readers.setdefault(k, {})[s] = (v, e)

    def op(self, e, fn, reads=(), writes=()):
        self._wait(e, self._deps(reads, writes), False)
        ins = fn(self.eng[e])
        self.cnt[e] += 1
        ins.then_inc(self.sem[e], 1)
        sid = id(self.sem[e])
        self.semobj[sid] = self.sem[e]
        self._record((sid, self.cnt[e], e), reads, writes)
        return ins

    def dma(self, q, out, in_, reads=(), writes=(), slot=None, multi=False, **kw):
        deps = {} if multi and False else self._deps(reads, writes if not multi else ())
        self._wait(q, deps, True)
        sl = self._slot(slot)
        sl[1] += 16
        ins = self.eng[q].dma_start(out=out, in_=in_, **kw)
        ins.then_inc(sl[0], 16)
        sid = id(sl[0])
        self.semobj[sid] = sl[0]
        self._record((sid, sl[1], "dma"), reads, writes, multi=multi)
        return ins

    def barrier(self):
        tickets = {}
        for e in self.COMPUTE:
            if self.cnt[e] > 0:
                sid = id(self.sem[e])
                self.semobj[sid] = self.sem[e]
                tickets[sid] = (self.cnt[e], e)
        for name, (s, c) in self.slots.items():
            if c > 0:
                tickets[id(s)] = (c, "dma")
                self.semobj[id(s)] = s
        for e in self.eng:
            self._wait(e, tickets, True)

    def finish(self, q="sp"):
        tickets = {}
        for name, (s, c) in self.slots.items():
            if c > 0:
                tickets[id(s)] = (c, "dma")
                self.semobj[id(s)] = s
        for e in self.COMPUTE:
            if self.cnt[e] > 0:
                sid = id(self.sem[e])
                self.semobj[sid] = self.sem[e]
                tickets[sid] = (self.cnt[e], e)
        self._wait(q, tickets, True)


PERM = np.concatenate(
    [np.concatenate([np.arange(64 * r, 64 * r + 64), np.arange(64 * (4 + r), 64 * (4 + r) + 64)]) for r in range(4)]
    + [np.arange(768, 896), np.arange(1024, 1152), np.arange(512, 640), np.arange(640, 768),
       np.arange(1816, 2072), np.arange(2072, 2328), np.arange(2840, 2856),
       np.arange(896, 1024), np.arange(1152, 1280), np.arange(1280, 1304),
       np.arange(1304, 1816), np.arange(2328, 2840), np.arange(2856, 3368)])
C_Q, C_KS, C_KW, C_KC, C_VC, C_QG, C_KG, C_AL = 0, 512, 640, 768, 896, 1024, 1280, 1536
C_TMA, C_ZN, C_VG, C_ZG = 1552, 1832, 2344, 2856


def t5_bucket_np(dist):
    n = np.maximum(dist, 0)
    nf = np.maximum(n, 1).astype(np.float32)
    large = 16 + (np.log(nf / np.float32(16)) / np.float32(math.log(128 / 16)) * np.float32(16)).astype(np.int32)
    large = np.minimum(large, 31)
    return np.where(n < 16, n, large)


def host_consts(S, rel_bias):
    c = {}
    k = np.arange(128)[:, None]
    q = np.arange(128)[None, :]
    bn = np.zeros((2, 128, 8, 128), np.float32)
    for dd in range(2):
        dist = q - k + 128 * dd
        val = rel_bias[t5_bucket_np(dist)]
        val = np.where((dist >= 0)[:, :, None], val, np.float32(NEG))
        bn[dd] = val.transpose(0, 2, 1)
    c["biasNear"] = bn
    cb = np.zeros((18, 128, 8, 128), np.float32)
    for m in range(18):
        dist = q - 16 * k - 31 + 128 * m
        val = rel_bias[t5_bucket_np(dist)]
        val = np.where((dist >= 0)[:, :, None], val, np.float32(NEG))
        cb[m] = val.transpose(0, 2, 1)
    c["cmpBias"] = cb
    c["band4"] = np.where(k > q, 0.0, NEG).astype(np.float32)
    c["ident"] = np.eye(128, dtype=np.float32)
    ncb = S // 16 - 1
    nct = (ncb + 127) // 128
    cs = np.arange(nct * 128)[:, None] * 16
    bs = np.arange(128)[None, :] * 64
    ov = np.clip(np.minimum(cs + 32, bs + 64) - np.maximum(cs, bs), 0, None).astype(np.float32) / 32.0
    ov[ncb:] = 0.0
    c["ovl"] = ov.reshape(nct, 128, 128).transpose(1, 0, 2).copy()
    kk = np.arange(S)[None, :]
    nn = np.arange(128)[:, None]
    c["Ffull"] = (kk // 64 == nn).astype(np.float32)
    xx = np.arange(256)[None, :] - 128
    qq = np.arange(128)[:, None]
    cur = np.where(qq < 64, 0, 1)
    tm = np.ones((128, 256), np.float32)
    ta = np.zeros((128, 256), np.float32)
    nonc = xx > cur
    f1 = xx == cur
    f2 = xx == cur - 1
    tm[nonc | f1 | f2] = 0.0
    ta[np.broadcast_to(nonc, ta.shape)] = -1e30
    ta[np.broadcast_to(f1, ta.shape)] = 1e30
    ta[np.broadcast_to(f2, ta.shape)] = 2e30
    c["topk_m"] = tm
    c["topk_a"] = ta
    tok = np.arange(512)
    c["seg01"] = np.broadcast_to((tok % 64 != 0).astype(np.float32)[None, :], (128, 512)).copy()
    s_ = np.arange(128)[:, None]
    c_ = np.arange(128)[None, :]
    c["tri"] = ((s_ // 64 == c_ // 64) & (c_ >= s_)).astype(np.float32)
    return c


def run_interleaved(gens, width, stagger=1):
    active = []
    it = iter(gens)
    rnd = 0
    last = -10 ** 9
    done = False
    while True:
        if not done and len(active) < width and rnd - last >= stagger:
            try:
                active.append(next(it))
                last = rnd
            except StopIteration:
                done = True
        if not active and done:
            break
        for g in list(active):
            try:
                next(g)
            except StopIteration:
                active.remove(g)
        rnd += 1


def build(S, dbg=False):
    NT = S // 128
    NG = S // 512
    NCB = S // 16 - 1
    NCT = (NCB + 127) // 128
    nc = bass.Bass("TRN2", target_bir_lowering=False)

    def din(name, shape, dt=F32):
        return nc.dram_tensor(name, list(shape), dt, kind="ExternalInput").ap()

    def dscr(name, shape, dt):
        return nc.dram_tensor(name, list(shape), dt, kind="ExternalOutput" if dbg else "Internal").ap()

    x = din("x", [S, D]); p_in = din("p", [S, 256]); w_in = din("w_in", [D, DIN])
    ln0gT = din("ln0gT", [128, 8]); ln0bT = din("ln0bT", [128, 8])
    ln0g_r = din("ln0g_r", [1, D]); ln0b_r = din("ln0b_r", [1, D])
    lng_r = din("lng_r", [1, D]); lnb_r = din("lnb_r", [1, D]); bpg_r = din("bpg_r", [1, D])
    rb31_r = din("rb31_r", [1, 8])
    biasNear = din("biasNear", [2, 128, 8, 128]); cmpBias = din("cmpBias", [18, 128, 8, 128])
    band4 = din("band4", [128, 128]); ident = din("ident", [128, 128])
    ovl_d = din("ovl", [128, NCT, 128]); Ffull_d = din("Ffull", [128, S])
    topk_m = din("topk_m", [128, 256]); topk_a = din("topk_a", [128, 256])
    seg01_d = din("seg01", [128, 512]); tri_d = din("tri", [128, 128])
    w_a2 = din("w_a2", [16, 256]); nbaT = din("baT", [128, 2]); gnw_r = din("gnw_r", [1, 128])
    posT = din("posT", [128, 32])
    w1_d = [din("w_ck1", [2048, 256]), din("w_cv1", [2048, 256])]
    b1T_d = [din("b_ck1T", [128, 2]), din("b_cv1T", [128, 2])]
    w2_d = [din("w_ck2", [256, 64]), din("w_cv2", [256, 64])]
    w_out = din("w_out", [D, D]); w_pe = din("w_pe", [256, D]); w_pg = din("w_pg", [D, D])
    out = nc.dram_tensor("out", [S, D], F32, kind="ExternalOutput").ap()

    qT_s = dscr("qT_s", [128, 4, S], BF16)
    cab_s = dscr("cab_s", [2, 2, 128, S], BF16)
    gates_s = dscr("gates_s", [S, 24], F32)
    zn_s = dscr("zn_s", [S, 512], F32)
    mixg_s = dscr("mixg_s", [S, 512], BF16)
    vsw_s = dscr("vsw_s", [128, 2, S // 128, 130], BF16)
    dbg_out = {}

    with ExitStack() as top:
        sc = Sched(nc, top)

        def sbt(es, name, shape, dt):
            return es.enter_context(nc.sbuf_tensor("sb_" + name, list(shape), dt))

        def pst(es, name, shape, dt=F32):
            return es.enter_context(nc.psum_tensor("ps_" + name, list(shape), dt))

        OP = sc.op

        def ln_generic(src, skey, s12, s12k, junk, jkey, mv, mvk, rstd, rk):
            OP("dve", lambda e: e.reduce_sum(out=s12[:, 0:1], in_=src[:], axis=AX.X), reads=[skey], writes=[s12k])
            OP("act", lambda e: e.activation(out=junk, in_=src[:], func=AF.Square, accum_out=s12[:, 1:2]),
               reads=[skey], writes=[s12k + "b", jkey])
            OP("dve", lambda e: e.tensor_scalar(out=mv[:, 0:1], in0=s12[:, 0:1], scalar1=1.0 / 1024.0, scalar2=None, op0=ALU.mult),
               reads=[s12k], writes=[mvk])
            OP("dve", lambda e: e.tensor_tensor(out=s12[:, 2:3], in0=mv[:, 0:1], in1=mv[:, 0:1], op=ALU.mult),
               reads=[mvk], writes=[s12k + "c"])
            OP("dve", lambda e: e.scalar_tensor_tensor(out=mv[:, 1:2], in0=s12[:, 1:2], scalar=1.0 / 1024.0, in1=s12[:, 2:3],
                                                       op0=ALU.mult, op1=ALU.subtract), reads=[s12k + "b", s12k + "c"], writes=[mvk])
            rsqrt_eps(rstd[:], mv[:, 1:2], 1.0, mvk, rk)

        def ln_gen(src, skey, s12, s12k, junk, jkey, mv, mvk, rstd, rk):
            OP("dve", lambda e: e.reduce_sum(out=s12[:, 0:1], in_=src[:], axis=AX.X), reads=[skey], writes=[s12k])
            OP("act", lambda e: e.activation(out=junk, in_=src[:], func=AF.Square, accum_out=s12[:, 1:2]),
               reads=[skey], writes=[s12k + "b", jkey])
            yield
            OP("dve", lambda e: e.tensor_scalar(out=mv[:, 0:1], in0=s12[:, 0:1], scalar1=1.0 / 1024.0, scalar2=None, op0=ALU.mult),
               reads=[s12k], writes=[mvk])
            OP("dve", lambda e: e.tensor_tensor(out=s12[:, 2:3], in0=mv[:, 0:1], in1=mv[:, 0:1], op=ALU.mult),
               reads=[mvk], writes=[s12k + "c"])
            yield
            OP("dve", lambda e: e.scalar_tensor_tensor(out=mv[:, 1:2], in0=s12[:, 1:2], scalar=1.0 / 1024.0, in1=s12[:, 2:3],
                                                       op0=ALU.mult, op1=ALU.subtract), reads=[s12k + "b", s12k + "c"], writes=[mvk])
            OP("dve", lambda e: e.tensor_scalar(out=rstd[:], in0=mv[:, 1:2], scalar1=1.0, scalar2=EPS, op0=ALU.mult, op1=ALU.add),
               reads=[mvk], writes=[rk])
            yield
            OP("act", lambda e: e.activation(out=rstd[:], in_=rstd[:], func=AF.Sqrt), reads=[rk], writes=[rk])
            yield
            OP("dve", lambda e: e.reciprocal(out=rstd[:], in_=rstd[:]), reads=[rk], writes=[rk])

        def rsqrt_eps(dst, src, scale, skey, dkey):
            OP("dve", lambda e: e.tensor_scalar(out=dst, in0=src, scalar1=scale, scalar2=EPS, op0=ALU.mult, op1=ALU.add),
               reads=[skey], writes=[dkey])
            OP("act", lambda e: e.activation(out=dst, in_=dst, func=AF.Sqrt), reads=[dkey], writes=[dkey])
            OP("dve", lambda e: e.reciprocal(out=dst, in_=dst), reads=[dkey], writes=[dkey])

        identf = sbt(top, "identf", [128, 128], F32)
        identb = sbt(top, "identb", [128, 128], BF16)
        g0T = sbt(top, "g0T", [128, 8], F32)
        b0T = sbt(top, "b0T", [128, 8], F32)
        pers = ExitStack()
        pers.__enter__()
        KsT = sbt(pers, "KsT", [128, S], BF16)
        KwT = sbt(pers, "KwT", [128, S], BF16)
        kcT = sbt(pers, "kcT", [128, NCT * 128], BF16)
        vcA = sbt(pers, "vcA", [128, NCT, 2, 65], BF16)

        sc.dma("sp", identf[:], ident, writes=["identf"], slot="c_identf")
        sc.dma("sp", g0T[:], ln0gT, writes=["g0T"], slot="c_g0T")
        sc.dma("sp", b0T[:], ln0bT, writes=["b0T"], slot="c_b0T")
        OP("dve", lambda e: e.tensor_copy(out=identb[:], in_=identf[:]), reads=["identf"], writes=["identb"])
        OP("dve", lambda e: e.memset(vcA[:], 1.0), writes=["vcA"])
        OP("dve", lambda e: e.memset(kcT[:], 0.0), writes=["kcT"])

        with ExitStack() as p1:
            Wb = sbt(p1, "Wb", [128, 8, DIN], BF16)
            with ExitStack() as pw:
                wst = sbt(pw, "wst", [128, DIN], F32)
                for c in range(8):
                    sc.dma("sp", wst[:], w_in[c * 128:(c + 1) * 128, :], writes=["wst"], slot="wst")
                    eng = ("dve", "pool")[c % 2]
                    OP(eng, lambda e, c=c: e.tensor_copy(out=Wb[:, c, :], in_=wst[:]), reads=["wst"], writes=["Wb"])
                sc.barrier()
            xbuf = [sbt(p1, f"xb{i}", [128, D], F32) for i in range(2)]
            xn = sbt(p1, "xn", [128, D], F32)
            stats = sbt(p1, "stats", [128, 12], F32)
            mv = sbt(p1, "mv", [128, 2], F32)
            rstd = sbt(p1, "rstd", [128, 1], F32)
            tmpf = sbt(p1, "tmpf", [128, 8, 128], F32)
            hT2 = [sbt(p1, f"hT{k}", [128, 8, 512], BF16) for k in range(2)]
            qst = sbt(p1, "qst", [128, 4, 512], BF16)
            cst = sbt(p1, "cst", [128, 2, 2, 512], BF16)
            posTs = sbt(p1, "posTs", [128, 32], F32)
            qgS2 = [sbt(p1, f"qgS{k}", [128, 2, 512], F32) for k in range(2)]
            kgS2 = [sbt(p1, f"kgS{k}", [128, 2, 512], F32) for k in range(2)]
            alT2 = [sbt(p1, f"alT{k}", [16, 512], BF16) for k in range(2)]
            vst = sbt(p1, "vst", [128, 2, 2, 65], BF16)
            OP("dve", lambda e: e.memset(vst[:], 1.0), writes=["vst"])
            gst = sbt(p1, "gst", [128, 24], F32)
            znst = sbt(p1, "znst", [128, 512], F32)
            vgS2 = [sbt(p1, f"vgS{k}", [128, 4, 512], BF16) for k in range(2)]
            zgS2 = [sbt(p1, f"zgS{k}", [128, 4, 512], F32) for k in range(2)]
            wa2f = sbt(p1, "wa2f", [16, 256], F32)
            wa2b = sbt(p1, "wa2b", [16, 256], BF16)
            nba = sbt(p1, "nba", [128, 2], F32)
            gnwB = sbt(p1, "gnwB", [128, 128], F32)
            seg01 = sbt(p1, "seg01", [128, 512], F32)
            trif = sbt(p1, "trif", [128, 128], F32)
            e1 = sbt(p1, "e1", [128, 512], F32)
            spb = sbt(p1, "spb", [128, 512], F32)
            cum = sbt(p1, "cum", [128, 512], F32)
            Eb = sbt(p1, "Eb", [128, 512], F32)
            Ei = sbt(p1, "Ei", [128, 512], F32)
            dl = e1
            EL = spb
            dec = sbt(p1, "dec", [128, 2, 8], F32)
            qeZ = [[sbt(p1, f"qeZ{hp}{h2}", [128, 512], BF16) for h2 in range(2)] for hp in range(2)]
            keT = sbt(p1, "keT", [128, 2, 512], BF16)
            klT = sbt(p1, "klT", [128, 2, 512], BF16)
            klZ4 = [[sbt(p1, f"klZ{hp}{ch}", [128, 128], BF16) for ch in range(2)] for hp in range(2)]
            Am4 = [sbt(p1, f"Am{h}", [128, 128], BF16) for h in range(4)]
            Sf = [sbt(p1, f"Sf{hp}", [128, 128], F32) for hp in range(2)]
            Sb = [[sbt(p1, f"Sb{hp}{i}", [128, 128], BF16) for i in range(3)] for hp in range(2)]
            oS = spb
            sq = e1
            ss = sbt(p1, "ss", [128, 4], F32)
            mixg = sbt(p1, "mixg", [128, 512], BF16)

            tp = pst(p1, "tp", [128, 8, 128], F32)
            psA = [pst(p1, f"psA{i}", [128, 512], F32) for i in range(2)]
            ops = pst(p1, "ops", [128, 512], F32)
            aps = pst(p1, "aps", [128, 4, 128], F32)
            kvps = pst(p1, "kvps", [128, 2, 256], F32)
            tpb2 = pst(p1, "tpb2", [128, 2, 128], BF16)

            for nm, src, dst in (("posTs", posT, posTs), ("wa2f", w_a2, wa2f), ("nba", nbaT, nba),
                                 ("seg01", seg01_d, seg01), ("trif", tri_d, trif)):
                sc.dma("sp", dst[:], src, writes=[nm], slot="c_" + nm)
            sc.dma("sp", gnwB[:], gnw_r.partition_broadcast(128), writes=["gnwB"], slot="c_gnwB")
            OP("dve", lambda e: e.tensor_copy(out=wa2b[:], in_=wa2f[:]), reads=["wa2f"], writes=["wa2b"])
            OP("dve", lambda e: e.tensor_scalar(out=nba[:], in0=nba[:], scalar1=-1.0, scalar2=None, op0=ALU.mult),
               reads=["nba"], writes=["nba"])
            for hp in range(2):
                OP("dve", lambda e, hp=hp: e.memset(Sf[hp][:], 0.0), writes=[f"Sf{hp}"])
                OP("dve", lambda e, hp=hp: e.memset(Sb[hp][0][:], 0.0), writes=[f"Sb{hp}0"])
                for h2 in range(2):
                    OP("dve", lambda e, hp=hp, h2=h2: e.memset(qeZ[hp][h2][:], 0.0), writes=[f"qeZ{hp}{h2}"])
            for hp in range(2):
                for ch in range(2):
                    OP("dve", lambda e, ch=ch, hp=hp: e.memset(klZ4[hp][ch][:], 0.0), writes=[f"klZ{hp}{ch}"])
            sbi = [0, 0]
            pk = [0]

            def nextps():
                pk[0] += 1
                return psA[pk[0] % 2], f"psA{pk[0] % 2}"

            s12 = sbt(p1, "s12", [128, 4], F32)

            def ln_tile(es_x, xb, xbk, mvk="mv"):
                ln_generic(xb, xbk, s12, "s12", tmpf[:].rearrange("p a b -> p (a b)"), "tmpf", mv, "mv", rstd, "rstd")

            def lnG(G):
                hT = hT2[G % 2]; hk = f"hT{G % 2}"
                for tt in range(4):
                    t = 4 * G + tt
                    xb = xbuf[t % 2]; xbk = f"xb{t % 2}"
                    sc.dma("sp", xb[:], x[t * 128:(t + 1) * 128, :], writes=[xbk], slot=xbk)
                    yield
                    yield from ln_gen(xb, xbk, s12, "s12", tmpf[:].rearrange("p a b -> p (a b)"), "tmpf", mv, "mv", rstd, "rstd")
                    yield
                    OP("dve", lambda e: e.tensor_scalar(out=xn[:], in0=xb[:], scalar1=mv[:, 0:1], scalar2=rstd[:, 0:1],
                                                        op0=ALU.subtract, op1=ALU.mult),
                       reads=[xbk, "mv", "rstd"], writes=["xn"])
                    yield
                    for c in range(8):
                        OP("pe", lambda e, c=c: e.transpose(tp[:, c, :], xn[:, c * 128:(c + 1) * 128], identf[:]),
                           reads=["xn", "identf"], writes=["tp"])
                    yield
                    for hf in range(2):
                        OP("dve", lambda e, hf=hf: e.tensor_tensor(
                            out=tmpf[:, 4 * hf:4 * hf + 4, :], in0=tp[:, 4 * hf:4 * hf + 4, :],
                            in1=g0T[:, 4 * hf:4 * hf + 4].unsqueeze(2).to_broadcast([128, 4, 128]), op=ALU.mult),
                           reads=["tp", "g0T"], writes=["tmpf"])
                    yield
                    OP("pool", lambda e, tt=tt: e.tensor_tensor(
                        out=hT[:, :, tt * 128:(tt + 1) * 128], in0=tmpf[:],
                        in1=b0T[:].unsqueeze(2).to_broadcast([128, 8, 128]), op=ALU.add),
                       reads=["tmpf", "b0T"], writes=[hk])
                    yield

            def frontG(G):
                hT = hT2[G % 2]; hk = f"hT{G % 2}"
                qgS = qgS2[G % 2]; kgS = kgS2[G % 2]; alT = alT2[G % 2]; vgS = vgS2[G % 2]; zgS = zgS2[G % 2]
                kq = f"qgS{G % 2}"; kk = f"kgS{G % 2}"; ka = f"alT{G % 2}"; kv = f"vgS{G % 2}"; kz = f"zgS{G % 2}"

                def proj_fm(col0, M):
                    ps, pkey = nextps()
                    for c in range(8):
                        OP("pe", lambda e, c=c: e.matmul(ps[0:M, :], lhsT=Wb[:, c, col0:col0 + M], rhs=hT[:, c, :],
                                                         start=(c == 0), stop=(c == 7)),
                           reads=["Wb", hk], writes=[pkey])
                        if c == 3:
                            yield
                    return ps, pkey

                yield
                tsl = slice(G * 512, (G + 1) * 512)
                for r in range(4):
                    ps, pkey = yield from proj_fm(C_Q + 128 * r, 128)
                    OP("act", lambda e, r=r, ps=ps: e.copy(out=qst[:, r, :], in_=ps[:]), reads=[pkey], writes=["qst"])
                sc.dma("sp", qT_s[:, :, tsl], qst[:], reads=["qst"], writes=["qT_s"], slot="qst", multi=True)
                ps, pkey = yield from proj_fm(C_KS, 128)
                OP("act", lambda e, ps=ps: e.copy(out=KsT[:, tsl], in_=ps[:]), reads=[pkey], writes=["KsT"])
                ps, pkey = yield from proj_fm(C_KW, 128)
                OP("act", lambda e, ps=ps: e.copy(out=KwT[:, tsl], in_=ps[:]), reads=[pkey], writes=["KwT"])
                for net, col in ((0, C_KC), (1, C_VC)):
                    ps, pkey = yield from proj_fm(col, 128)
                    for ab in range(2):
                        OP("dve", lambda e, ps=ps, net=net, ab=ab: e.tensor_tensor(
                            out=cst[:, net, ab, :].rearrange("p (j l) -> p j l", l=16),
                            in0=ps[:].rearrange("p (j l) -> p j l", l=16),
                            in1=posTs[:, 16 * ab:16 * ab + 16].unsqueeze(1).to_broadcast([128, 32, 16]), op=ALU.add),
                           reads=[pkey, "posTs"], writes=["cst"])
                for net in range(2):
                    for ab in range(2):
                        sc.dma("sp", cab_s[net, ab, :, tsl], cst[:, net, ab, :], reads=["cst"], writes=["cab_s"],
                               slot="cst", multi=True)
                for hp in range(2):
                    ps, pkey = yield from proj_fm(C_QG + 128 * hp, 128)
                    OP("act", lambda e, ps=ps, hp=hp: e.copy(out=qgS[:, hp, :], in_=ps[:]), reads=[pkey], writes=[kq])
                    ps, pkey = yield from proj_fm(C_KG + 128 * hp, 128)
                    OP("act", lambda e, ps=ps, hp=hp: e.copy(out=kgS[:, hp, :], in_=ps[:]), reads=[pkey], writes=[kk])
                ps, pkey = yield from proj_fm(C_AL, 16)
                OP("act", lambda e, ps=ps: e.copy(out=alT[:], in_=ps[0:16, :]), reads=[pkey], writes=[ka])

                for tt in range(4):
                    t = 4 * G + tt

                    def proj_tm(col0, N):
                        ps, pkey = nextps()
                        for c in range(8):
                            OP("pe", lambda e, c=c: e.matmul(ps[:, 0:N], lhsT=hT[:, c, tt * 128:(tt + 1) * 128],
                                                             rhs=Wb[:, c, col0:col0 + N], start=(c == 0), stop=(c == 7)),
                               reads=["Wb", hk], writes=[pkey])
                            if c == 3:
                                yield
                        return ps, pkey

                    ps, pkey = yield from proj_tm(C_TMA, 280)
                    OP("act", lambda e, ps=ps: e.copy(out=vst[:, :, :, 0:64].rearrange("p w g d -> p (w g) d"),
                                                     in_=ps[:, 0:256].rearrange("p (a d) -> p a d", d=64)),
                       reads=[pkey], writes=["vst"])
                    sc.dma("sp", vsw_s[:, :, t, :], vst[:].rearrange("p w g d -> p w (g d)"), reads=["vst"], writes=["vsw_s"],
                           slot="vst", multi=True)
                    OP("act", lambda e, ps=ps: e.activation(out=gst[:], in_=ps[:, 256:280], func=AF.Tanh, scale=0.5),
                       reads=[pkey], writes=["gst"])
                    OP("dve", lambda e: e.tensor_scalar(out=gst[:], in0=gst[:], scalar1=0.5, scalar2=0.5, op0=ALU.mult, op1=ALU.add),
                       reads=["gst"], writes=["gst"])
                    sc.dma("sp", gates_s[t * 128:(t + 1) * 128, :], gst[:], reads=["gst"], writes=["gates_s"],
                           slot="gst", multi=True)
                    ps, pkey = yield from proj_tm(C_ZN, 512)
                    OP("act", lambda e, ps=ps: e.activation(out=znst[:], in_=ps[:], func=AF.Silu),
                       reads=[pkey], writes=["znst"])
                    sc.dma("sp", zn_s[t * 128:(t + 1) * 128, :], znst[:], reads=["znst"], writes=["zn_s"],
                           slot="znst", multi=True)
                    ps, pkey = yield from proj_tm(C_VG, 512)
                    OP("act", lambda e, ps=ps, tt=tt: e.copy(out=vgS[:, tt, :], in_=ps[:]), reads=[pkey], writes=[kv])
                    ps, pkey = yield from proj_tm(C_ZG, 512)
                    OP("act", lambda e, ps=ps, tt=tt: e.activation(out=zgS[:, tt, :], in_=ps[:], func=AF.Silu),
                       reads=[pkey], writes=[kz])
                    OP("pool", lambda e, tt=tt: e.tensor_tensor(
                        out=zgS[:, tt, :].rearrange("p (h v) -> p h v", h=4),
                        in0=zgS[:, tt, :].rearrange("p (h v) -> p h v", h=4),
                        in1=gnwB[:].unsqueeze(1).to_broadcast([128, 4, 128]), op=ALU.mult),
                       reads=[kz, "gnwB"], writes=[kz])

            def glaG(G):
                qgS = qgS2[G % 2]; kgS = kgS2[G % 2]; alT = alT2[G % 2]; vgS = vgS2[G % 2]; zgS = zgS2[G % 2]
                kq = f"qgS{G % 2}"; kk = f"kgS{G % 2}"; ka = f"alT{G % 2}"; kv = f"vgS{G % 2}"; kz = f"zgS{G % 2}"
                for hp in range(2):
                    yield
                    ps, pkey = nextps()
                    OP("pe", lambda e, ps=ps, hp=hp: e.matmul(ps[:], lhsT=wa2b[0:16, hp * 128:(hp + 1) * 128],
                                                             rhs=alT[0:16, :], start=True, stop=True),
                       reads=["wa2b", ka], writes=[pkey])
                    OP("act", lambda e, ps=ps, hp=hp: e.activation(out=e1[:], in_=ps[:], func=AF.Exp,
                                                                  bias=nba[:, hp:hp + 1], scale=-1.0),
                       reads=[pkey, "nba"], writes=["e1"])
                    yield
                    OP("act", lambda e: e.activation(out=spb[:], in_=e1[:], func=AF.Ln, bias=1.0, scale=1.0),
                       reads=["e1"], writes=["spb"])
                    yield
                    OP("dve", lambda e, hp=hp: e.tensor_tensor_scan(out=cum[:], data0=seg01[:], data1=spb[:],
                                                                   initial=0.0, op0=ALU.mult, op1=ALU.add),
                       reads=["seg01", "spb"], writes=["cum"])
                    yield
                    OP("act", lambda e, hp=hp: e.activation(out=Eb[:], in_=cum[:], func=AF.Exp,
                                                           scale=-1.0 / 16.0), reads=["cum"], writes=["Eb"])
                    OP("act", lambda e, hp=hp: e.activation(out=Ei[:], in_=cum[:], func=AF.Exp,
                                                           scale=1.0 / 16.0), reads=["cum"], writes=["Ei"])
                    yield
                    OP("dve", lambda e, hp=hp: e.tensor_tensor(
                        out=dl[:].rearrange("p (c s) -> p c s", s=64),
                        in0=cum[:].rearrange("p (c s) -> p c s", s=64),
                        in1=cum[:].rearrange("p (c s) -> p c s", s=64)[:, :, 63:64].to_broadcast([128, 8, 64]),
                        op=ALU.subtract), reads=["cum"], writes=["e1"])
                    OP("act", lambda e: e.activation(out=EL[:], in_=dl[:], func=AF.Exp, scale=1.0 / 16.0),
                       reads=["e1"], writes=["spb"])
                    OP("dve", lambda e, hp=hp: e.tensor_copy(
                        out=dec[:, hp, :], in_=Eb[:].rearrange("p (c s) -> p c s", s=64)[:, :, 63]),
                       reads=["Eb"], writes=["dec"])
                    yield
                    for h2 in range(2):
                        rs = slice(64 * h2, 64 * h2 + 64)
                        OP("dve", lambda e, hp=hp, h2=h2, rs=rs: e.scalar_tensor_tensor(
                            out=qeZ[hp][h2][rs, :], in0=qgS[rs, hp, :], scalar=0.125, in1=Eb[rs, :],
                            op0=ALU.mult, op1=ALU.mult), reads=[kq, "Eb"], writes=[f"qeZ{hp}{h2}"])
                    OP("pool", lambda e, hp=hp: e.tensor_tensor(out=keT[:, hp, :], in0=kgS[:, hp, :], in1=Ei[:], op=ALU.mult),
                       reads=[kk, "Ei"], writes=["keT"])
                    OP("pool", lambda e, hp=hp: e.tensor_tensor(out=klT[:, hp, :], in0=kgS[:, hp, :], in1=EL[:], op=ALU.mult),
                       reads=[kk, "spb"], writes=["klT"])
                for tt in range(4):
                    t = 4 * G + tt
                    tk = slice(tt * 128, (tt + 1) * 128)
                    yield
                    for hp in range(2):
                        OP("pe", lambda e, hp=hp: e.transpose(tpb2[:, hp, :], klT[:, hp, tk], identb[:]),
                           reads=["klT", "identb"], writes=["tpb2"])
                    for hp in range(2):
                        for ch in range(2):
                            rs = slice(64 * ch, 64 * ch + 64)
                            OP("act", lambda e, ch=ch, rs=rs, hp=hp: e.copy(out=klZ4[hp][ch][rs, :], in_=tpb2[rs, hp, :]),
                               reads=["tpb2"], writes=[f"klZ{hp}{ch}"])
                    yield
                    sas = [sbi[0], sbi[1]]
                    for hp in range(2):
                        sa = sas[hp]
                        for ch in range(2):
                            OP("pe", lambda e, ch=ch, hp=hp: e.matmul(
                                kvps[:, ch, :], lhsT=klZ4[hp][ch][:], rhs=vgS[:, tt, hp * 256:(hp + 1) * 256],
                                start=True, stop=True), reads=[f"klZ{hp}{ch}", kv], writes=["kvps"])
                        yield
                        for ch in range(2):
                            cidx = 2 * tt + ch
                            for h2 in range(2):
                                rs = slice(64 * h2, 64 * h2 + 64)
                                OP("dve", lambda e, hp=hp, h2=h2, rs=rs, cidx=cidx, ch=ch: e.scalar_tensor_tensor(
                                    out=Sf[hp][rs, :], in0=Sf[hp][rs, :], scalar=dec[rs, hp, cidx:cidx + 1],
                                    in1=kvps[rs, ch, 128 * h2:128 * h2 + 128], op0=ALU.mult, op1=ALU.add),
                                   reads=[f"Sf{hp}", "dec", "kvps"], writes=[f"Sf{hp}"])
                            nb = (sa + 1 + ch) % 3
                            OP("act", lambda e, hp=hp, nb=nb: e.copy(out=Sb[hp][nb][:], in_=Sf[hp][:]),
                               reads=[f"Sf{hp}"], writes=[f"Sb{hp}{nb}"])
                        yield
                    for hp in range(2):
                        for h2 in range(2):
                            h = 2 * hp + h2
                            OP("pe", lambda e, hp=hp, h2=h2, h=h: e.matmul(aps[:, h, :], lhsT=keT[:, hp, tk],
                                                                           rhs=qeZ[hp][h2][:, tk], start=True, stop=True),
                               reads=["keT", f"qeZ{hp}{h2}"], writes=["aps"])
                    yield
                    for h in range(4):
                        OP("dve", lambda e, h=h: e.tensor_tensor(out=Am4[h][:], in0=aps[:, h, :], in1=trif[:], op=ALU.mult),
                           reads=["aps", "trif"], writes=[f"Am{h}"])
                    yield
                    first = True
                    for hp in range(2):
                        sa = sas[hp]
                        for h2 in range(2):
                            h = 2 * hp + h2
                            OP("pe", lambda e, h=h, first=first: e.matmul(
                                ops[:, h * 128:(h + 1) * 128], lhsT=Am4[h][:], rhs=vgS[:, tt, h * 128:(h + 1) * 128],
                                start=first, stop=False, skip_group_check=True), reads=[f"Am{h}", kv], writes=["ops"])
                            first = False
                            for ch in range(2):
                                sbk = (sa + ch) % 3
                                OP("pe", lambda e, h=h, hp=hp, h2=h2, ch=ch, sbk=sbk: e.matmul(
                                    ops[64 * ch:64 * ch + 64, h * 128:(h + 1) * 128],
                                    lhsT=qeZ[hp][h2][:, tt * 128 + 64 * ch:tt * 128 + 64 * ch + 64],
                                    rhs=Sb[hp][sbk][:], start=False, stop=(ch == 1), skip_group_check=True),
                                   reads=[f"qeZ{hp}{h2}", f"Sb{hp}{sbk}"], writes=["ops"])
                        sbi[hp] = (sa + 2) % 3
                    yield
                    OP("act", lambda e: e.copy(out=oS[:], in_=ops[:]), reads=["ops"], writes=["spb"])
                    yield
                    OP("pool", lambda e: e.tensor_tensor(out=sq[:], in0=oS[:], in1=oS[:], op=ALU.mult),
                       reads=["spb"], writes=["e1"])
                    OP("dve", lambda e: e.reduce_sum(out=ss[:], in_=sq[:].rearrange("p (h v) -> p h v", h=4), axis=AX.X),
                       reads=["e1"], writes=["ss"])
                    rsqrt_eps(ss[:], ss[:], 1.0 / 128.0, "ss", "ss")
                    OP("pool", lambda e: e.tensor_tensor(
                        out=sq[:].rearrange("p (h v) -> p h v", h=4), in0=oS[:].rearrange("p (h v) -> p h v", h=4),
                        in1=ss[:].unsqueeze(2).to_broadcast([128, 4, 128]), op=ALU.mult),
                       reads=["spb", "ss"], writes=["e1"])
                    OP("pool", lambda e, tt=tt: e.tensor_tensor(out=mixg[:], in0=sq[:], in1=zgS[:, tt, :], op=ALU.mult),
                       reads=["e1", kz], writes=["mixg"])
                    sc.dma("sp", mixg_s[t * 128:(t + 1) * 128, :], mixg[:], reads=["mixg"], writes=["mixg_s"],
                           slot="mixg", multi=True)

            def zip_many(gens):
                act_ = [g for g in gens if g is not None]
                while act_:
                    for g in list(act_):
                        try:
                            next(g)
                        except StopIteration:
                            act_.remove(g)

            zip_many([lnG(0)])
            for G in range(NG):
                zip_many([frontG(G), lnG(G + 1) if G + 1 < NG else None, glaG(G - 1) if G >= 1 else None])
            zip_many([glaG(NG - 1)])
            sc.barrier()

        pre2 = ExitStack()
        pre2.__enter__()
        Vs = sbt(pre2, "Vs", [128, NT, 2, 65], BF16)
        Vw = sbt(pre2, "Vw", [128, NT, 2, 65], BF16)
        sc.dma("sp", Vs[:].rearrange("p t g d -> p (t g d)"), vsw_s[:, 0, :, :].rearrange("p t c -> p (t c)"),
               reads=["vsw_s"], writes=["Vs"], slot="Vs")
        sc.dma("sp", Vw[:].rearrange("p t g d -> p (t g d)"), vsw_s[:, 1, :, :].rearrange("p t c -> p (t c)"),
               reads=["vsw_s"], writes=["Vw"], slot="Vw")
        fst = sbt(pre2, "fst", [128, 2048], F32)
        Fb = sbt(pre2, "Fb", [128, S], BF16)
        for c in range((S + 2047) // 2048):
            w = min(2048, S - c * 2048)
            sc.dma("sp", fst[:, 0:w], Ffull_d[:, c * 2048:c * 2048 + w], writes=["fst"], slot="fst")
            OP("dve", lambda e, c=c, w=w: e.tensor_copy(out=Fb[:, c * 2048:c * 2048 + w], in_=fst[:, 0:w]),
               reads=["fst"], writes=["Fb"])
        with ExitStack() as p1b:
            cA = sbt(p1b, "cA", [128, S], BF16)
            cB = sbt(p1b, "cB", [128, S], BF16)
            imc = sbt(p1b, "imc", [128, 32, S // 16], BF16)
            w1st2 = [sbt(p1b, f"w1st{k}", [128, 8, 256], F32) for k in range(2)]
            w1b = sbt(p1b, "w1b", [128, 32, 256], BF16)
            w2st = sbt(p1b, "w2st", [128, 2, 64], F32)
            w2b = sbt(p1b, "w2b", [128, 2, 64], BF16)
            b1T = sbt(p1b, "b1T", [128, 2], F32)
            xg = sbt(p1b, "xg", [128, 512], F32)
            ug = sbt(p1b, "ug", [128, 512], F32)
            sg = sbt(p1b, "sg", [128, 512], F32)
            h1T = sbt(p1b, "h1T", [128, 2, 512], BF16)
            hps = [pst(p1b, f"hps{i}", [128, 512], F32) for i in range(4)]
            ps2 = pst(p1b, "ps2", [128, 512], F32)
            ps3 = pst(p1b, "ps3", [128, 64], F32)
            OP("dve", lambda e: e.memset(h1T[:], 0.0), writes=["h1T"])
            for net in range(2):
                sc.dma("sp", cA[:], cab_s[net, 0], reads=["cab_s"], writes=["cA"], slot="cA")
                sc.dma("sp", cB[:], cab_s[net, 1], reads=["cab_s"], writes=["cB"], slot="cB")
                w1v = w1_d[net].rearrange("(l d) h -> d l h", d=64)
                for lc in range(4):
                    w1st = w1st2[lc % 2]; wk_ = f"w1st{lc % 2}"
                    for half in range(2):
                        sc.dma("sp", w1st[64 * half:64 * half + 64, :, :], w1v[:, 8 * lc:8 * lc + 8, :],
                               writes=[wk_], slot=wk_)
                    OP(("dve", "pool")[lc % 2], lambda e, lc=lc, w1st=w1st: e.tensor_copy(out=w1b[:, 8 * lc:8 * lc + 8, :], in_=w1st[:]),
                       reads=[wk_], writes=["w1b"])
                sc.dma("sp", w2st[:], w2_d[net].rearrange("(c p) d -> p c d", p=128), writes=["w2st"], slot="w2st")
                OP("dve", lambda e: e.tensor_copy(out=w2b[:], in_=w2st[:]), reads=["w2st"], writes=["w2b"])
                sc.dma("sp", b1T[:], b1T_d[net], writes=["b1T"], slot="b1T")
                cAv = cA[:].rearrange("p (j l) -> p j l", l=16)
                cBv = cB[:].rearrange("p (j l) -> p j l", l=16)
                for l in range(32):
                    src_ = cAv[:, 0:NCB, l] if l < 16 else cBv[:, 1:NCB + 1, l - 16]
                    OP(("dve", "dve", "pool")[l % 3], lambda e, l=l, src_=src_: e.tensor_copy(out=imc[:, l, 0:NCB], in_=src_),
                       reads=["cA", "cB"], writes=[f"imc{l}"])
                for g in range(2):
                    rs = slice(64 * g, 64 * g + 64)
                    for hc in range(2):
                        ps = hps[2 * g + hc]; pkey = f"hps{2 * g + hc}"
                        for l in range(32):
                            rhs = imc[rs, l, 0:NCB]
                            OP("pe", lambda e, ps=ps, l=l, rhs=rhs, hc=hc, rs=rs: e.matmul(
                                ps[:, 0:NCB], lhsT=w1b[rs, l, hc * 128:(hc + 1) * 128], rhs=rhs,
                                start=(l == 0), stop=(l == 31)), reads=["w1b", f"imc{l}"], writes=[pkey])
                        OP("act", lambda e, ps=ps, hc=hc: e.activation(out=xg[:, 0:NCB], in_=ps[:, 0:NCB], func=AF.Identity,
                                                                      bias=b1T[:, hc:hc + 1], scale=1.0),
                           reads=[pkey, "b1T"], writes=["xg"])
                        OP("pool", lambda e: e.tensor_tensor(out=ug[:, 0:NCB], in0=xg[:, 0:NCB], in1=xg[:, 0:NCB], op=ALU.mult),
                           reads=["xg"], writes=["ug"])
                        OP("dve", lambda e: e.tensor_scalar(out=ug[:, 0:NCB], in0=ug[:, 0:NCB], scalar1=0.044715, scalar2=1.0,
                                                            op0=ALU.mult, op1=ALU.add), reads=["ug"], writes=["ug"])
                        OP("pool", lambda e: e.tensor_tensor(out=ug[:, 0:NCB], in0=ug[:, 0:NCB], in1=xg[:, 0:NCB], op=ALU.mult),
                           reads=["xg", "ug"], writes=["ug"])
                        OP("act", lambda e: e.activation(out=sg[:, 0:NCB], in_=ug[:, 0:NCB], func=AF.Sigmoid,
                                                         scale=2.0 * math.sqrt(2.0 / math.pi)), reads=["ug"], writes=["sg"])
                        OP("pool", lambda e, hc=hc: e.tensor_tensor(out=h1T[:, hc, 0:NCB], in0=xg[:, 0:NCB], in1=sg[:, 0:NCB],
                                                                   op=ALU.mult), reads=["xg", "sg"], writes=["h1T"])
                    if net == 0:
                        for hc in range(2):
                            OP("pe", lambda e, hc=hc, rs=rs: e.matmul(ps2[rs, 0:NCB], lhsT=w2b[:, hc, :], rhs=h1T[:, hc, 0:NCB],
                                                                      start=(hc == 0), stop=(hc == 1)),
                               reads=["w2b", "h1T"], writes=["ps2"])
                        OP("act", lambda e, rs=rs: e.copy(out=kcT[rs, 0:NCB], in_=ps2[rs, 0:NCB]), reads=["ps2"], writes=["kcT"])
                    else:
                        for j in range(NCT):
                            for hc in range(2):
                                OP("pe", lambda e, hc=hc, j=j: e.matmul(ps3[:], lhsT=h1T[:, hc, j * 128:(j + 1) * 128],
                                                                        rhs=w2b[:, hc, :], start=(hc == 0), stop=(hc == 1)),
                                   reads=["w2b", "h1T"], writes=["ps3"])
                            OP("act", lambda e, j=j, g=g: e.copy(out=vcA[:, j, g, 0:64], in_=ps3[:]), reads=["ps3"], writes=["vcA"])
            sc.barrier()

        if dbg:
            for nm, t_, shp in (("KsT", KsT, [128, S]), ("KwT", KwT, [128, S]), ("kcT", kcT, [128, NCT * 128]),
                                ("vcA", vcA, [128, NCT, 2, 65])):
                d_ = nc.dram_tensor("dbg_" + nm, shp, BF16, kind="ExternalOutput").ap()
                sc.dma("sp", d_, t_[:], slot="dbg_" + nm)
            sc.barrier()
        mixn_s = dscr("mixn_s", [S, 512], BF16)
        with ExitStack() as p2:
            bnear = sbt(p2, "bnear", [128, 2, 8, 128], F32)
            for dd in range(2):
                sc.dma("sp", bnear[:, dd, :, :], biasNear[dd], writes=["bnear"], slot="bnear")
            cmpB = [[sbt(p2, f"cmpB{s}{k}", [128, 8, 128], F32) for k in range(2)] for s in range(2)]
            bn8 = sbt(p2, "bn8", [128, 2, 8, 128], F32)
            bnH = sbt(p2, "bnH", [128, 2, 8, 128], BF16)
            bnL = sbt(p2, "bnL", [128, 2, 8, 128], BF16)
            OP("dve", lambda e: e.tensor_scalar(out=bn8[:], in0=bnear[:], scalar1=8.0, scalar2=None, op0=ALU.mult),
               reads=["bnear"], writes=["bn8"])
            OP("dve", lambda e: e.tensor_copy(out=bnH[:], in_=bn8[:]), reads=["bn8"], writes=["bnH"])
            OP("dve", lambda e: e.tensor_tensor(out=bn8[:], in0=bn8[:], in1=bnH[:], op=ALU.subtract),
               reads=["bn8", "bnH"], writes=["bn8"])
            OP("dve", lambda e: e.tensor_copy(out=bnL[:], in_=bn8[:]), reads=["bn8"], writes=["bnL"])
            ovlf = sbt(p2, "ovlf", [128, NCT, 128], F32)
            ovlb = sbt(p2, "ovlb", [128, NCT, 128], BF16)
            sc.dma("sp", ovlf[:], ovl_d, writes=["ovlf"], slot="c_ovlf")
            OP("dve", lambda e: e.tensor_copy(out=ovlb[:], in_=ovlf[:]), reads=["ovlf"], writes=["ovlb"])
            Tm = sbt(p2, "Tm", [128, 256], F32); Ta = sbt(p2, "Ta", [128, 256], F32)
            sc.dma("sp", Tm[:], topk_m, writes=["Tm"], slot="c_Tm")
            sc.dma("sp", Ta[:], topk_a, writes=["Ta"], slot="c_Ta")
            c8b = sbt(p2, "c8b", [128, 8], F32)
            sc.dma("sp", c8b[:], rb31_r.partition_broadcast(128), writes=["c8b"], slot="c_c8b")
            OP("dve", lambda e: e.tensor_scalar(out=c8b[:], in0=c8b[:], scalar1=8.0, scalar2=None, op0=ALU.mult),
               reads=["c8b"], writes=["c8b"])
            onesrow = sbt(p2, "onesrow", [128, 128], BF16)
            OP("dve", lambda e: e.memset(onesrow[:], 0.0), writes=["onesrow"])
            OP("dve", lambda e: e.memset(onesrow[0:1, :], 1.0), writes=["onesrow"])
            c8rhs = [sbt(p2, f"c8rhs{g}", [128, 4, 128], BF16) for g in range(2)]
            B4 = [sbt(p2, f"B4{g}", [128, 4, 128], BF16) for g in range(2)]
            b4f = sbt(p2, "b4f", [128, 128], F32)
            sc.dma("sp", b4f[:], band4, writes=["b4f"], slot="c_b4f")
            for g in range(2):
                OP("dve", lambda e, g=g: e.memset(c8rhs[g][:], 0.0), writes=[f"c8rhs{g}"])
                OP("dve", lambda e, g=g: e.tensor_copy(out=c8rhs[g][0:1, :, :],
                                                       in_=c8b[0:1, 4 * g:4 * g + 4].unsqueeze(2).to_broadcast([1, 4, 128])),
                   reads=["c8b"], writes=[f"c8rhs{g}"])
                OP("dve", lambda e, g=g: e.tensor_tensor(out=B4[g][:], in0=b4f[:].unsqueeze(1).to_broadcast([128, 4, 128]),
                                                         in1=c8b[:, 4 * g:4 * g + 4].unsqueeze(2).to_broadcast([128, 4, 128]),
                                                         op=ALU.add), reads=["b4f", "c8b"], writes=[f"B4{g}"])
            QTz = [[sbt(p2, f"QTz{s}{g}", [128, 4, 128], BF16) for g in range(2)] for s in range(2)]
            for s in range(2):
                for g in range(2):
                    OP("dve", lambda e, s=s, g=g: e.memset(QTz[s][g][:], 0.0), writes=[f"QTz{s}{g}"])
            Pb = [sbt(p2, f"Pb{i}", [128, 512], BF16) for i in range(5)]
            tmpb = [sbt(p2, f"tmpb{i}", [128, 512], F32) for i in range(2)]
            ocn = [sbt(p2, f"ocn{s}", [128, 8, 64], F32) for s in range(2)]
            osn = sbt(p2, "osn", [128, 8, 64], F32)
            own = sbt(p2, "own", [128, 8, 64], F32)
            MnT = [[sbt(p2, f"MnT{s}{g}", [128, 4, 128], BF16) for g in range(2)] for s in range(2)]
            MnR = [[sbt(p2, f"MnR{s}{g}", [128, 4, 128], BF16) for g in range(2)] for s in range(2)]
            rz = sbt(p2, "rz", [128, 4], F32)
            tU = sbt(p2, "tU", [128, 4, 128], F32)
            imp = sbt(p2, "imp", [128, 128], F32)
            t2 = sbt(p2, "t2", [128, 128], F32)
            t3 = sbt(p2, "t3", [128, 128], F32)
            m1 = sbt(p2, "m1", [128, 8], F32)
            m2 = sbt(p2, "m2", [128, 8], F32)
            mq = sbt(p2, "mq", [128, 128], BF16)
            gt = sbt(p2, "gt", [128, 24], F32)
            znt = sbt(p2, "znt", [128, 512], F32)
            acc = sbt(p2, "acc", [128, 8, 64], F32)
            tacc = sbt(p2, "tacc", [128, 8, 64], F32)
            mixn = sbt(p2, "mixn", [128, 512], BF16)

            ST = [pst(p2, f"ST{i}", [128, 512], F32) for i in range(3)]
            YY = pst(p2, "YY", [128, 2, 512], F32)
            OSp = pst(p2, "OSp", [128, 512], F32)
            OWp = pst(p2, "OWp", [128, 512], F32)
            TT = pst(p2, "TT", [128, 128], BF16)
            rot = {"st": 0, "p": 0, "t": 0}

            def nxt(kind, n):
                rot[kind] = (rot[kind] + 1) % n
                return rot[kind]

            def s_tile(lhsT, lkeys, sl, g, second, near_bias):
                si = nxt("st", 3); st = ST[si]; stk = f"ST{si}"
                qk = f"QTz{sl}{g}"
                extras = [] if second is None else (second if isinstance(second, list) else [second])
                OP("pe", lambda e: e.matmul(st[:], lhsT=lhsT, rhs=QTz[sl][g][:].rearrange("p r q -> p (r q)"),
                                            start=True, stop=(not extras)), reads=lkeys + [qk], writes=[stk])
                for xi, (l2, r2, k2) in enumerate(extras):
                    OP("pe", lambda e, l2=l2, r2=r2, xi=xi: e.matmul(st[:], lhsT=l2, rhs=r2, start=False,
                                                                     stop=(xi == len(extras) - 1)), reads=k2, writes=[stk])
                pi = nxt("p", 5); P = Pb[pi]; pk_ = f"Pb{pi}"
                if near_bias is None:
                    OP("act", lambda e: e.activation(out=P[:], in_=st[:], func=AF.Exp, scale=0.125), reads=[stk], writes=[pk_])
                else:
                    bap, bkey = near_bias
                    ti = nxt("t", 2); tb = tmpb[ti]; tk_ = f"tmpb{ti}"
                    OP("dve", lambda e: e.scalar_tensor_tensor(out=tb[:].rearrange("p (r q) -> p r q", r=4),
                                                               in0=st[:].rearrange("p (r q) -> p r q", r=4), scalar=0.125,
                                                               in1=bap, op0=ALU.mult, op1=ALU.add),
                       reads=[stk, bkey], writes=[tk_])
                    OP("act", lambda e: e.activation(out=P[:], in_=tb[:], func=AF.Exp), reads=[tk_], writes=[pk_])
                return P, pk_

            def pv(P, pk_, acc_ps, akey, width, rhs, rkeys, first):
                for r in range(4):
                    OP("pe", lambda e, r=r: e.matmul(acc_ps[:, r * width:(r + 1) * width], lhsT=P[:, r * 128:(r + 1) * 128],
                                                     rhs=rhs, start=(first and r == 0), stop=False, skip_group_check=True),
                       reads=[pk_] + rkeys, writes=[akey])

            pend_pv = []
            pend_late = []

            def tick():
                for q_ in (pend_pv, pend_late):
                    for it in q_:
                        it[0] -= 1
                    while q_ and q_[0][0] <= 0:
                        q_.pop(0)[1]()

            def flush_all():
                while pend_pv or pend_late:
                    for q_ in (pend_pv, pend_late):
                        while q_:
                            q_.pop(0)[1]()

            def unit(s_args, pv_list):
                P, pk_ = s_tile(*s_args)
                tick()
                pend_pv.append([2, lambda: [pv(P, pk_, *a_) for a_ in pv_list]])

            mqs = [[sbt(p2, f"mq{s_}{g}", [128, 128], BF16) for g in range(2)] for s_ in range(2)]
            gts = [sbt(p2, f"gt{s_}", [128, 24], F32) for s_ in range(2)]
            znts = [sbt(p2, f"znt{s_}", [128, 512], F32) for s_ in range(2)]

            def finalizeA(i, g, sl):
                OC = YY[:, 0, 0:260]; U = YY[:, 1, :]
                OCv = OC.rearrange("p (r d) -> p r d", r=4)
                mqk = f"mq{sl}{g}"; mqb = mqs[sl][g]
                OP("dve", lambda e: e.tensor_scalar(out=rz[:], in0=OCv[:, :, 64], scalar1=1e-30, scalar2=None, op0=ALU.max),
                   reads=["YY0"], writes=["rz"])
                OP("dve", lambda e: e.reciprocal(out=rz[:], in_=rz[:]), reads=["rz"], writes=["rz"])
                OP("dve", lambda e: e.tensor_tensor(out=ocn[sl][:, 4 * g:4 * g + 4, :], in0=OCv[:, :, 0:64],
                                                    in1=rz[:].unsqueeze(2).to_broadcast([128, 4, 64]), op=ALU.mult),
                   reads=["YY0", "rz"], writes=[f"ocn{sl}"])
                OP("dve", lambda e: e.tensor_tensor(out=tU[:], in0=U.rearrange("p (r n) -> p r n", r=4),
                                                    in1=rz[:].unsqueeze(2).to_broadcast([128, 4, 128]), op=ALU.mult),
                   reads=["YY1", "rz"], writes=["tU"])
                OP("dve", lambda e: e.reduce_sum(out=imp[:], in_=tU[:].rearrange("p r n -> p n r"), axis=AX.X),
                   reads=["tU"], writes=["imp"])
                lo = 128 - 2 * i
                OP("pool", lambda e: e.tensor_tensor(out=t2[:], in0=imp[:], in1=Tm[:, lo:lo + 128], op=ALU.mult),
                   reads=["imp", "Tm"], writes=["t2"])
                OP("pool", lambda e: e.tensor_tensor(out=t2[:], in0=t2[:], in1=Ta[:, lo:lo + 128], op=ALU.add),
                   reads=["t2", "Ta"], writes=["t2"])
                OP("pool", lambda e: e.memset(t2[:, 0:1], 3e30), writes=["t2"])
                OP("dve", lambda e: e.max(out=m1[:], in_=t2[:]), reads=["t2"], writes=["m1"])
                OP("dve", lambda e: e.match_replace(out=t3[:], in_to_replace=m1[:], in_values=t2[:], imm_value=-3e38),
                   reads=["t2", "m1"], writes=["t3"])
                OP("dve", lambda e: e.max(out=m2[:], in_=t3[:]), reads=["t3"], writes=["m2"])
                OP("dve", lambda e: e.tensor_scalar(out=mqb[:], in0=t2[:], scalar1=m2[:, 7:8], scalar2=NEG,
                                                    op0=ALU.is_lt, op1=ALU.mult), reads=["t2", "m2"], writes=[mqk])

            def maskbuild(i, g, sl):
                mqk = f"mq{sl}{g}"; mqb = mqs[sl][g]
                OP("pe", lambda e: e.transpose(TT[:], mqb[:], identb[:]), reads=[mqk, "identb"], writes=["TT"])
                OP("dve", lambda e: e.tensor_copy(out=MnR[sl][g][:], in_=TT[:].unsqueeze(1).to_broadcast([128, 4, 128])),
                   reads=["TT"], writes=[f"MnR{sl}{g}"])
                OP("dve", lambda e: e.tensor_tensor(out=MnT[sl][g][:], in0=TT[:].unsqueeze(1).to_broadcast([128, 4, 128]),
                                                    in1=c8b[:, 4 * g:4 * g + 4].unsqueeze(2).to_broadcast([128, 4, 128]),
                                                    op=ALU.add), reads=["TT", "c8b"], writes=[f"MnT{sl}{g}"])

            def stageA_load(i):
                sl = i % 2
                for g in range(2):
                    rs = slice(64 * g, 64 * g + 64)
                    sc.dma("sp", QTz[sl][g][rs, :, :], qT_s[rs, :, i * 128:(i + 1) * 128],
                           writes=[f"QTz{sl}{g}"], slot=f"QTz{sl}{g}")
                js = [j for j in range(NCT) if i - 16 * j >= 0]
                nearmap = {}
                for j in js:
                    m = i - 16 * j
                    if m <= 17:
                        k = len(nearmap)
                        nearmap[j] = k
                        sc.dma("sp", cmpB[sl][k][:], cmpBias[m], writes=[f"cmpB{sl}{k}"], slot=f"cmpB{sl}{k}")
                return js, nearmap

            def stageA_g(i, g, js, nearmap):
                sl = i % 2
                OC = YY[:, 0, 0:260]; U = YY[:, 1, :]
                first = True
                for j in js:
                    lhsT = kcT[:, j * 128:(j + 1) * 128]
                    if j in nearmap:
                        k = nearmap[j]
                        sargs = (lhsT, ["kcT"], sl, g, None, (cmpB[sl][k][:, 4 * g:4 * g + 4, :], f"cmpB{sl}{k}"))
                    else:
                        sargs = (lhsT, ["kcT"], sl, g,
                                 (onesrow[:], c8rhs[g][:].rearrange("p r q -> p (r q)"), ["onesrow", f"c8rhs{g}"]), None)
                    unit(sargs, [(OC, "YY0", 65, vcA[:, j, g, :], ["vcA"], first),
                                 (U, "YY1", 128, ovlb[:, j, :], ["ovlb"], first)])
                    first = False
                pend_pv.append([2, lambda: finalizeA(i, g, sl)])
                pend_late.append([20, lambda: maskbuild(i, g, sl), i])

            def finalizeB1(g, psx, key, dst, dk):
                v = psx[:, 0:260].rearrange("p (r d) -> p r d", r=4)
                OP("dve", lambda e: e.reciprocal(out=rz[:], in_=v[:, :, 64]), reads=[key], writes=["rz"])
                OP("dve", lambda e: e.tensor_tensor(
                    out=dst[:, 4 * g:4 * g + 4, :], in0=v[:, :, 0:64],
                    in1=rz[:].unsqueeze(2).to_broadcast([128, 4, 64]), op=ALU.mult), reads=[key, "rz"], writes=[dk])

            def combine(i, sl):
                tsl = slice(i * 128, (i + 1) * 128)
                gt = gts[sl]; znt = znts[sl]
                gv = gt[:].rearrange("p (h b) -> p h b", b=3)
                OP("pool", lambda e: e.tensor_tensor(out=acc[:], in0=ocn[sl][:], in1=gv[:, :, 0:1].to_broadcast([128, 8, 64]),
                                                     op=ALU.mult), reads=[f"ocn{sl}", f"gt{sl}"], writes=["acc"])
                for src, sk, b_ in ((osn, "osn", 1), (own, "own", 2)):
                    OP("pool", lambda e, src=src, b_=b_: e.tensor_tensor(out=tacc[:], in0=src[:],
                                                                         in1=gv[:, :, b_:b_ + 1].to_broadcast([128, 8, 64]), op=ALU.mult),
                       reads=[sk, f"gt{sl}"], writes=["tacc"])
                    OP("pool", lambda e: e.tensor_tensor(out=acc[:], in0=acc[:], in1=tacc[:], op=ALU.add),
                       reads=["acc", "tacc"], writes=["acc"])
                OP("pool", lambda e: e.tensor_tensor(out=mixn[:], in0=acc[:].rearrange("p h d -> p (h d)"), in1=znt[:], op=ALU.mult),
                   reads=["acc", f"znt{sl}"], writes=["mixn"])
                sc.dma("sp", mixn_s[tsl, :], mixn[:], reads=["mixn"], writes=["mixn_s"], slot="mixn", multi=True)

            def stageB_pre(i):
                sl = i % 2
                tsl = slice(i * 128, (i + 1) * 128)
                if pend_late and pend_late[0][2] == i:
                    while pend_pv:
                        pend_pv.pop(0)[1]()
                    while pend_late and pend_late[0][2] == i:
                        pend_late.pop(0)[1]()
                sc.dma("sp", gts[sl][:], gates_s[tsl, :], reads=["gates_s"], writes=[f"gt{sl}"], slot=f"gt{sl}")
                sc.dma("sp", znts[sl][:], zn_s[tsl, :], reads=["zn_s"], writes=[f"znt{sl}"], slot=f"znt{sl}")

            def stageB_g(i, g):
                sl = i % 2
                first = True
                for j in range(0, i + 1):
                    dd = i - j
                    lhsT = KsT[:, j * 128:(j + 1) * 128]
                    Fl = Fb[:, j * 128:(j + 1) * 128]
                    if dd >= 2:
                        sargs = (lhsT, [], sl, g, (Fl, MnT[sl][g][:].rearrange("p r q -> p (r q)"), [f"MnT{sl}{g}", "Fb"]), None)
                    else:
                        sargs = (lhsT, [], sl, g, [(Fl, MnR[sl][g][:].rearrange("p r q -> p (r q)"), [f"MnR{sl}{g}", "Fb"]),
                                                   (identb[:], bnH[:, dd, 4 * g:4 * g + 4, :], ["identb", "bnH"]),
                                                   (identb[:], bnL[:, dd, 4 * g:4 * g + 4, :], ["identb", "bnL"])], None)
                    unit(sargs, [(OSp, "OSp", 65, Vs[:, j, g, :], [], first)])
                    first = False
                pend_pv.append([2, lambda: finalizeB1(g, OSp, "OSp", osn, "osn")])
                first = True
                for j in range(max(0, i - 4), i + 1):
                    dd = i - j
                    lhsT = KwT[:, j * 128:(j + 1) * 128]
                    if dd == 4:
                        sargs = (lhsT, [], sl, g, (identb[:], B4[g][:].rearrange("p r q -> p (r q)"), [f"B4{g}", "identb"]), None)
                    elif dd >= 2:
                        sargs = (lhsT, [], sl, g,
                                 (onesrow[:], c8rhs[g][:].rearrange("p r q -> p (r q)"), ["onesrow", f"c8rhs{g}"]), None)
                    else:
                        sargs = (lhsT, [], sl, g, [(identb[:], bnH[:, dd, 4 * g:4 * g + 4, :], ["identb", "bnH"]),
                                                   (identb[:], bnL[:, dd, 4 * g:4 * g + 4, :], ["identb", "bnL"])], None)
                    unit(sargs, [(OWp, "OWp", 65, Vw[:, j, g, :], [], first)])
                    first = False
                pend_pv.append([2, lambda: finalizeB1(g, OWp, "OWp", own, "own")])

            info = stageA_load(0)
            for g in range(2):
                stageA_g(0, g, *info)
            for i in range(NT):
                if i + 1 < NT:
                    info = stageA_load(i + 1)
                stageB_pre(i)
                for g in range(2):
                    if i + 1 < NT:
                        stageA_g(i + 1, g, *info)
                    stageB_g(i, g)
                pend_pv.append([2, lambda i=i: combine(i, i % 2)])
            flush_all()
            sc.barrier()

        pre2.close()
        pers.close()
        with ExitStack() as p3:
            wst3 = sbt(p3, "wst3", [128, 1024], F32)
            woutb = sbt(p3, "woutb", [128, 8, 1024], BF16)
            wpgb = sbt(p3, "wpgb", [128, 8, 1024], BF16)
            wpeb = sbt(p3, "wpeb", [128, 2, 1024], BF16)
            k = 0
            for wsrc, wdst, nck, key in ((w_out, woutb, 8, "woutb"), (w_pg, wpgb, 8, "wpgb"), (w_pe, wpeb, 2, "wpeb")):
                for c in range(nck):
                    sc.dma("sp", wst3[:], wsrc[c * 128:(c + 1) * 128, :], writes=["wst3"], slot="wst3")
                    OP(("dve", "pool")[k % 2], lambda e, c=c, wdst=wdst: e.tensor_copy(out=wdst[:, c, :], in_=wst3[:]),
                       reads=["wst3"], writes=[key])
                    k += 1
            bt = {}
            for nm, src in (("g0B", ln0g_r), ("b0B", ln0b_r), ("lngB", lng_r), ("lnbB", lnb_r), ("bpgB", bpg_r)):
                bt[nm] = sbt(p3, nm, [128, 1024], F32)
                sc.dma("sp", bt[nm][:], src.partition_broadcast(128), writes=[nm], slot="c_" + nm)
            NS3 = 5
            ag0B = sbt(p3, "ag0B", [128, 1024], F32)
            ab0B = sbt(p3, "ab0B", [128, 1024], F32)
            OP("dve", lambda e: e.tensor_scalar(out=ag0B[:], in0=bt["g0B"][:], scalar1=ALPHA, scalar2=None, op0=ALU.mult),
               reads=["g0B"], writes=["ag0B"])
            OP("dve", lambda e: e.tensor_scalar(out=ab0B[:], in0=bt["b0B"][:], scalar1=ALPHA, scalar2=None, op0=ALU.mult),
               reads=["b0B"], writes=["ab0B"])
            T3s = [(pst(p3, f"T3_{k}", [128, 8, 128], BF16), f"T3_{k}") for k in range(2)]
            t3rot = [0]

            def mkset(k):
                d = {}
                for nm, shp, dt_ in (("x3", [128, 1024], F32), ("mn3", [128, 1024], BF16), ("pt3", [128, 256], F32),
                                     ("ptb", [128, 256], BF16), ("pT3", [128, 2, 128], BF16), ("mixT", [128, 8, 128], BF16),
                                     ("s123", [128, 4], F32), ("mv3", [128, 2], F32), ("rstd3", [128, 1], F32),
                                     ("nmr3", [128, 1], F32),
                                     ("hh", [128, 1024], F32), ("rr", [128, 1024], F32), ("rbf", [128, 1024], BF16),
                                     ("rT", [128, 8, 128], BF16), ("gg", [128, 1024], F32)):
                    d[nm] = (sbt(p3, f"{nm}_{k}", shp, dt_), f"{nm}_{k}")
                d["xo"] = d["hh"]
                d["Y3"] = (pst(p3, f"Y3_{k}", [128, 512], F32), f"Y3_{k}")
                return d

            sets3 = [mkset(k) for k in range(NS3)]

            def tile3(i):
                B_ = sets3[i % NS3]
                Y3, Y3k = B_["Y3"]
                xb, xk = B_["x3"]; mn3, mnk = B_["mn3"]; pt3, ptk = B_["pt3"]; ptb, ptbk = B_["ptb"]
                pT3, pTk = B_["pT3"]; mixT, mixTk = B_["mixT"]; s123, s123k = B_["s123"]; mv3, mv3k = B_["mv3"]
                rstd3, rstd3k = B_["rstd3"]; hh, hhk = B_["hh"]; rr, rrk = B_["rr"]; rbf, rbfk = B_["rbf"]
                rT, rTk = B_["rT"]; gg, ggk = B_["gg"]; xob, xok = B_["xo"]; nmr, nmrk = B_["nmr3"]

                def lnstats(src, skey):
                    OP("dve", lambda e: e.reduce_sum(out=s123[:, 0:1], in_=src[:], axis=AX.X), reads=[skey], writes=[s123k]); yield
                    OP("act", lambda e: e.activation(out=gg[:], in_=src[:], func=AF.Square, accum_out=s123[:, 1:2]),
                       reads=[skey], writes=[s123k + "b", ggk]); yield
                    OP("dve", lambda e: e.tensor_scalar(out=mv3[:, 0:1], in0=s123[:, 0:1], scalar1=1.0 / 1024.0, scalar2=None,
                                                        op0=ALU.mult), reads=[s123k], writes=[mv3k]); yield
                    OP("dve", lambda e: e.tensor_tensor(out=s123[:, 2:3], in0=mv3[:, 0:1], in1=mv3[:, 0:1], op=ALU.mult),
                       reads=[mv3k], writes=[s123k + "c"]); yield
                    OP("dve", lambda e: e.scalar_tensor_tensor(out=mv3[:, 1:2], in0=s123[:, 1:2], scalar=1.0 / 1024.0,
                                                               in1=s123[:, 2:3], op0=ALU.mult, op1=ALU.subtract),
                       reads=[s123k + "b", s123k + "c"], writes=[mv3k]); yield
                    OP("dve", lambda e: e.tensor_scalar(out=rstd3[:], in0=mv3[:, 1:2], scalar1=1.0, scalar2=EPS,
                                                        op0=ALU.mult, op1=ALU.add), reads=[mv3k], writes=[rstd3k]); yield
                    OP("act", lambda e: e.activation(out=rstd3[:], in_=rstd3[:], func=AF.Sqrt), reads=[rstd3k], writes=[rstd3k]); yield
                    OP("dve", lambda e: e.reciprocal(out=rstd3[:], in_=rstd3[:]), reads=[rstd3k], writes=[rstd3k]); yield
                    OP("dve", lambda e: e.scalar_tensor_tensor(out=nmr[:], in0=mv3[:, 0:1], scalar=-1.0, in1=rstd3[:],
                                                               op0=ALU.mult, op1=ALU.mult), reads=[mv3k, rstd3k], writes=[nmrk]); yield

                def transposes(src, skey, n, dst, dkey):
                    t3rot[0] = (t3rot[0] + 1) % 2
                    T3, T3k = T3s[t3rot[0]]
                    for c in range(n):
                        OP("pe", lambda e, c=c: e.transpose(T3[:, c, :], src[:, c * 128:(c + 1) * 128], identb[:]),
                           reads=[skey, "identb"], writes=[T3k])
                    OP("act", lambda e: e.copy(out=dst[:, 0:n, :], in_=T3[:, 0:n, :]), reads=[T3k], writes=[dkey]); yield

                def mmh(lhs, lkey, wb, wkey, nck, half):
                    for c in range(nck):
                        OP("pe", lambda e, c=c: e.matmul(Y3[:], lhsT=lhs[:, c, :],
                                                         rhs=wb[:, c, half * 512:(half + 1) * 512],
                                                         start=(c == 0), stop=(c == nck - 1)),
                           reads=[lkey, wkey], writes=[Y3k])
                        if c % 4 == 3:
                            yield
                    yield

                H = lambda ap, half: ap[:, half * 512:(half + 1) * 512]
                v2 = lambda ap: ap.rearrange("p (a b) -> p a b", a=2)
                tsl = slice(i * 128, (i + 1) * 128)
                sc.dma("sp", xb[:], x[tsl, :], writes=[xk], slot=xk)
                sc.dma("sp", mn3[:, 0:512], mixn_s[tsl, :], reads=["mixn_s"], writes=[mnk], slot=mnk)
                sc.dma("sp", mn3[:, 512:1024], mixg_s[tsl, :], reads=["mixg_s"], writes=[mnk], slot=mnk)
                sc.dma("sp", pt3[:], p_in[tsl, :], writes=[ptk], slot=ptk)
                yield
                yield from transposes(mn3, mnk, 8, mixT, mixTk)
                yield from lnstats(xb, xk)
                OP("act", lambda e: e.copy(out=ptb[:], in_=pt3[:]), reads=[ptk], writes=[ptbk]); yield
                OP("act", lambda e: e.activation(out=hh[:], in_=xb[:], func=AF.Identity, bias=nmr[:, 0:1], scale=rstd3[:, 0:1]),
                   reads=[xk, nmrk, rstd3k], writes=[hhk]); yield
                OP("dve", lambda e: e.tensor_tensor(out=hh[:], in0=hh[:], in1=ag0B[:], op=ALU.mult),
                   reads=[hhk, "ag0B"], writes=[hhk]); yield
                for half in range(2):
                    yield from mmh(mixT, mixTk, woutb, "woutb", 8, half)
                    OP("dve", lambda e, half=half: e.tensor_tensor(out=H(hh, half), in0=H(hh, half), in1=Y3[:], op=ALU.add),
                       reads=[hhk, Y3k], writes=[hhk]); yield
                OP("pool", lambda e: e.tensor_tensor(out=rr[:], in0=hh[:], in1=ab0B[:], op=ALU.add),
                   reads=[hhk, "ab0B"], writes=[rrk]); yield
                OP("act", lambda e: e.copy(out=rbf[:], in_=rr[:]), reads=[rrk], writes=[rbfk]); yield
                yield from transposes(rbf, rbfk, 8, rT, rTk)
                for half in range(2):
                    yield from mmh(rT, rTk, wpgb, "wpgb", 8, half)
                    OP("dve", lambda e, half=half: e.tensor_tensor(out=H(gg, half), in0=Y3[:], in1=H(bt["bpgB"], half), op=ALU.add),
                       reads=[Y3k, "bpgB"], writes=[ggk]); yield
                OP("act", lambda e: e.activation(out=gg[:], in_=gg[:], func=AF.Sigmoid), reads=[ggk], writes=[ggk]); yield
                yield from transposes(ptb, ptbk, 2, pT3, pTk)
                for half in range(2):
                    yield from mmh(pT3, pTk, wpeb, "wpeb", 2, half)
                    OP("dve", lambda e, half=half: e.tensor_tensor(out=H(gg, half), in0=H(gg, half), in1=Y3[:], op=ALU.mult),
                       reads=[ggk, Y3k], writes=[ggk]); yield
                OP("pool", lambda e: e.tensor_tensor(out=rr[:], in0=rr[:], in1=gg[:], op=ALU.add), reads=[rrk, ggk], writes=[rrk]); yield
                yield from lnstats(rr, rrk)
                OP("act", lambda e: e.activation(out=xob[:], in_=rr[:], func=AF.Identity, bias=nmr[:, 0:1], scale=rstd3[:, 0:1]),
                   reads=[rrk, nmrk, rstd3k], writes=[xok]); yield
                OP("dve", lambda e: e.tensor_tensor(out=xob[:], in0=xob[:], in1=bt["lngB"][:], op=ALU.mult),
                   reads=[xok, "lngB"], writes=[xok]); yield
                OP("pool", lambda e: e.tensor_tensor(out=xob[:], in0=xob[:], in1=bt["lnbB"][:], op=ALU.add),
                   reads=[xok, "lnbB"], writes=[xok]); yield
                sc.dma("sp", out[tsl, :], xob[:], reads=[xok], writes=["out"], slot=xok, multi=True)
                yield

            run_interleaved((tile3(i) for i in range(NT)), NS3, stagger=11)
            sc.finish("sp")
            sc.finish("pool")
    return nc


_CACHE = {}
DBG = False
LAST = None


def make_inputs(S, b, inp, consts):
    f = lambda a: np.ascontiguousarray(a, dtype=np.float32)
    m = {
        "x": f(inp["x"][b]), "p": f(inp["p"][0, b]), "w_in": f(inp["w_in"][0][:, PERM]),
        "ln0gT": f(inp["ln0_g"].reshape(8, 128).T), "ln0bT": f(inp["ln0_b"].reshape(8, 128).T),
        "ln0g_r": f(inp["ln0_g"].reshape(1, D)), "ln0b_r": f(inp["ln0_b"].reshape(1, D)),
        "lng_r": f(inp["ln_g"][0].reshape(1, D)), "lnb_r": f(inp["ln_b"][0].reshape(1, D)),
        "bpg_r": f(inp["b_pg"][0].reshape(1, D)), "rb31_r": f(inp["rel_bias"][31].reshape(1, 8)),
        "w_a2": f(inp["w_a2"][0]), "baT": f(inp["b_a"][0].reshape(2, 128).T),
        "gnw_r": f(inp["gla_norm_w"][0].reshape(1, 128)),
        "posT": f(np.concatenate([inp["pos_cmp"][0].T, inp["pos_cmp"][0].T], axis=0)),
        "w_ck1": f(inp["w_ck1"][0]), "w_cv1": f(inp["w_cv1"][0]),
        "b_ck1T": f(inp["b_ck1"][0].reshape(2, 128).T), "b_cv1T": f(inp["b_cv1"][0].reshape(2, 128).T),
        "w_ck2": f(inp["w_ck2"][0]), "w_cv2": f(inp["w_cv2"][0]),
        "w_out": f(inp["w_out"][0]), "w_pe": f(inp["w_pe"][0]), "w_pg": f(inp["w_pg"][0]),
    }
    m.update(consts)
    return m


def kernel(**inputs):
    inp = {k: np.asarray(v) for k, v in inputs.items()}
    B, S, _ = inp["x"].shape
    if S not in _CACHE:
        _CACHE[S] = build(S, dbg=DBG)
    nc = _CACHE[S]
    consts = {k: np.ascontiguousarray(v, dtype=np.float32) for k, v in host_consts(S, inp["rel_bias"].astype(np.float32)).items()}
    in_maps = [make_inputs(S, b, inp, consts) for b in range(B)]
    res = run_bass_kernel_spmd(nc, in_maps, core_ids=list(range(B)))
    if DBG:
        global LAST
        LAST = res.results
    return np.stack([np.asarray(r["out"]) for r in res.results], axis=0).astype(np.float32)
```

```python
import math
from contextlib import ExitStack

import numpy as np
import concourse.bass as bass
import concourse.mybir as mybir
from concourse.bass_utils import run_bass_kernel_spmd

F32 = mybir.dt.float32
BF16 = mybir.dt.bfloat16
AF = mybir.ActivationFunctionType
ALU = mybir.AluOpType
AX = mybir.AxisListType

D = 1024
DIN = 3368
NEG = -30000.0
EPS = 1e-5
ALPHA = 2.0 ** 0.25


class Sched:
    COMPUTE = ("pe", "act", "dve", "pool")

    def __init__(self, nc, es):
        self.nc = nc
        self.es = es
        self.eng = {"pe": nc.tensor, "act": nc.scalar, "dve": nc.vector,
                    "pool": nc.gpsimd, "sp": nc.sync}
        self.sem = {}
        self.cnt = {}
        for e in self.COMPUTE:
            self.sem[e] = es.enter_context(nc.semaphore("s_" + e))
            self.cnt[e] = 0
        self.known = {e: {} for e in self.eng}
        self.lastw = {}
        self.readers = {}
        self.slots = {}
        self.semobj = {}

    def _slot(self, name):
        if name not in self.slots:
            s = self.es.enter_context(self.nc.semaphore("d_" + name))
            self.slots[name] = [s, 0]
        return self.slots[name]

    def _deps(self, reads, writes):
        d = {}
        for k in reads:
            for s, (v, e) in self.lastw.get(k, {}).items():
                if s not in d or d[s][0] < v:
                    d[s] = (v, e)
        for k in writes:
            for src in (self.lastw.get(k, {}), self.readers.get(k, {})):
                for s, (v, e) in src.items():
                    if s not in d or d[s][0] < v:
                        d[s] = (v, e)
        return d

    def _wait(self, e, deps, is_dma):
        for s, (v, src) in deps.items():
            if (not is_dma) and src == e and e == "pe":
                continue
            if self.known[e].get(s, 0) >= v:
                continue
            self.eng[e].wait_ge(self.semobj[s], v)
            self.known[e][s] = v

    def _record(self, t, reads, writes, multi=False):
        s, v, e = t
        for k in writes:
            if multi:
                self.lastw.setdefault(k, {})[s] = (v, e)
            else:
                self.lastw[k] = {s: (v, e)}
                self.readers[k] = {}
        for k in reads:
            self.readers.setdefault(k, {})[s] = (v, e)

    def op(self, e, fn, reads=(), writes=()):
        self._wait(e, self._deps(reads, writes), False)
        ins = fn(self.eng[e])
        self.cnt[e] += 1
        ins.then_inc(self.sem[e], 1)
        sid = id(self.sem[e])
        self.semobj[sid] = self.sem[e]
        self._record((sid, self.cnt[e], e), reads, writes)
        return ins

    def dma(self, q, out, in_, reads=(), writes=(), slot=None, multi=False, **kw):
        deps = {} if multi and False else self._deps(reads, writes if not multi else ())
        self._wait(q, deps, True)
        sl = self._slot(slot)
        sl[1] += 16
        ins = self.eng[q].dma_start(out=out, in_=in_, **kw)
        ins.then_inc(sl[0], 16)
        sid = id(sl[0])
        self.semobj[sid] = sl[0]
        self._record((sid, sl[1], "dma"), reads, writes, multi=multi)
        return ins

    def barrier(self):
        tickets = {}
        for e in self.COMPUTE:
            if self.cnt[e] > 0:
                sid = id(self.sem[e])
                self.semobj[sid] = self.sem[e]
                tickets[sid] = (self.cnt[e], e)
        for name, (s, c) in self.slots.items():
            if c > 0:
                tickets[id(s)] = (c, "dma")
                self.semobj[id(s)] = s
        for e in self.eng:
            self._wait(e, tickets, True)

    def finish(self, q="sp"):
        tickets = {}
        for name, (s, c) in self.slots.items():
            if c > 0:
                tickets[id(s)] = (c, "dma")
                self.semobj[id(s)] = s
        for e in self.COMPUTE:
            if self.cnt[e] > 0:
                sid = id(self.sem[e])
                self.semobj[sid] = self.sem[e]
                tickets[sid] = (self.cnt[e], e)
        self._wait(q, tickets, True)


PERM = np.concatenate(
    [np.concatenate([np.arange(64 * r, 64 * r + 64), np.arange(64 * (4 + r), 64 * (4 + r) + 64)]) for r in range(4)]
    + [np.arange(768, 896), np.arange(1024, 1152), np.arange(512, 640), np.arange(640, 768),
       np.arange(1816, 2072), np.arange(2072, 2328), np.arange(2840, 2856),
       np.arange(896, 1024), np.arange(1152, 1280), np.arange(1280, 1304),
       np.arange(1304, 1816), np.arange(2328, 2840), np.arange(2856, 3368)])
C_Q, C_KS, C_KW, C_KC, C_VC, C_QG, C_KG, C_AL = 0, 512, 640, 768, 896, 1024, 1280, 1536
C_TMA, C_ZN, C_VG, C_ZG = 1552, 1832, 2344, 2856


def t5_bucket_np(dist):
    n = np.maximum(dist, 0)
    nf = np.maximum(n, 1).astype(np.float32)
    large = 16 + (np.log(nf / np.float32(16)) / np.float32(math.log(128 / 16)) * np.float32(16)).astype(np.int32)
    large = np.minimum(large, 31)
    return np.where(n < 16, n, large)


def host_consts(S, rel_bias):
    c = {}
    k = np.arange(128)[:, None]
    q = np.arange(128)[None, :]
    bn = np.zeros((2, 128, 8, 128), np.float32)
    for dd in range(2):
        dist = q - k + 128 * dd
        val = rel_bias[t5_bucket_np(dist)]
        val = np.where((dist >= 0)[:, :, None], val, np.float32(NEG))
        bn[dd] = val.transpose(0, 2, 1)
    c["biasNear"] = bn
    cb = np.zeros((18, 128, 8, 128), np.float32)
    for m in range(18):
        dist = q - 16 * k - 31 + 128 * m
        val = rel_bias[t5_bucket_np(dist)]
        val = np.where((dist >= 0)[:, :, None], val, np.float32(NEG))
        cb[m] = val.transpose(0, 2, 1)
    c["cmpBias"] = cb
    c["band4"] = np.where(k > q, 0.0, NEG).astype(np.float32)
    c["ident"] = np.eye(128, dtype=np.float32)
    ncb = S // 16 - 1
    nct = (ncb + 127) // 128
    cs = np.arange(nct * 128)[:, None] * 16
    bs = np.arange(128)[None, :] * 64
    ov = np.clip(np.minimum(cs + 32, bs + 64) - np.maximum(cs, bs), 0, None).astype(np.float32) / 32.0
    ov[ncb:] = 0.0
    c["ovl"] = ov.reshape(nct, 128, 128).transpose(1, 0, 2).copy()
    kk = np.arange(S)[None, :]
    nn = np.arange(128)[:, None]
    c["Ffull"] = (kk // 64 == nn).astype(np.float32)
    xx = np.arange(256)[None, :] - 128
    qq = np.arange(128)[:, None]
    cur = np.where(qq < 64, 0, 1)
    tm = np.ones((128, 256), np.float32)
    ta = np.zeros((128, 256), np.float32)
    nonc = xx > cur
    f1 = xx == cur
    f2 = xx == cur - 1
    tm[nonc | f1 | f2] = 0.0
    ta[np.broadcast_to(nonc, ta.shape)] = -1e30
    ta[np.broadcast_to(f1, ta.shape)] = 1e30
    ta[np.broadcast_to(f2, ta.shape)] = 2e30
    c["topk_m"] = tm
    c["topk_a"] = ta
    tok = np.arange(512)
    c["seg01"] = np.broadcast_to((tok % 64 != 0).astype(np.float32)[None, :], (128, 512)).copy()
    s_ = np.arange(128)[:, None]
    c_ = np.arange(128)[None, :]
    c["tri"] = ((s_ // 64 == c_ // 64) & (c_ >= s_)).astype(np.float32)
    return c


def run_interleaved(gens, width, stagger=1):
    active = []
    it = iter(gens)
    rnd = 0
    last = -10 ** 9
    done = False
    while True:
        if not done and len(active) < width and rnd - last >= stagger:
            try:
                active.append(next(it))
                last = rnd
            except StopIteration:
                done = True
        if not active and done:
            break
        for g in list(active):
            try:
                next(g)
            except StopIteration:
                active.remove(g)
        rnd += 1


def build(S, dbg=False):
    NT = S // 128
    NG = S // 512
    NCB = S // 16 - 1
    NCT = (NCB + 127) // 128
    nc = bass.Bass("TRN2", target_bir_lowering=False)

    def din(name, shape, dt=F32):
        return nc.dram_tensor(name, list(shape), dt, kind="ExternalInput").ap()

    def dscr(name, shape, dt):
        return nc.dram_tensor(name, list(shape), dt, kind="ExternalOutput" if dbg else "Internal").ap()

    x = din("x", [S, D]); p_in = din("p", [S, 256]); w_in = din("w_in", [D, DIN])
    ln0gT = din("ln0gT", [128, 8]); ln0bT = din("ln0bT", [128, 8])
    ln0g_r = din("ln0g_r", [1, D]); ln0b_r = din("ln0b_r", [1, D])
    lng_r = din("lng_r", [1, D]); lnb_r = din("lnb_r", [1, D]); bpg_r = din("bpg_r", [1, D])
    rb31_r = din("rb31_r", [1, 8])
    biasNear = din("biasNear", [2, 128, 8, 128]); cmpBias = din("cmpBias", [18, 128, 8, 128])
    band4 = din("band4", [128, 128]); ident = din("ident", [128, 128])
    ovl_d = din("ovl", [128, NCT, 128]); Ffull_d = din("Ffull", [128, S])
    topk_m = din("topk_m", [128, 256]); topk_a = din("topk_a", [128, 256])
    seg01_d = din("seg01", [128, 512]); tri_d = din("tri", [128, 128])
    w_a2 = din("w_a2", [16, 256]); nbaT = din("baT", [128, 2]); gnw_r = din("gnw_r", [1, 128])
    posT = din("posT", [128, 32])
    w1_d = [din("w_ck1", [2048, 256]), din("w_cv1", [2048, 256])]
    b1T_d = [din("b_ck1T", [128, 2]), din("b_cv1T", [128, 2])]
    w2_d = [din("w_ck2", [256, 64]), din("w_cv2", [256, 64])]
    w_out = din("w_out", [D, D]); w_pe = din("w_pe", [256, D]); w_pg = din("w_pg", [D, D])
    out = nc.dram_tensor("out", [S, D], F32, kind="ExternalOutput").ap()

    qT_s = dscr("qT_s", [128, 4, S], BF16)
    cab_s = dscr("cab_s", [2, 2, 128, S], BF16)
    gates_s = dscr("gates_s", [S, 24], F32)
    zn_s = dscr("zn_s", [S, 512], F32)
    mixg_s = dscr("mixg_s", [S, 512], BF16)
    vsw_s = dscr("vsw_s", [128, 2, S // 128, 130], BF16)
    dbg_out = {}

    with ExitStack() as top:
        sc = Sched(nc, top)

        def sbt(es, name, shape, dt):
            return es.enter_context(nc.sbuf_tensor("sb_" + name, list(shape), dt))

        def pst(es, name, shape, dt=F32):
            return es.enter_context(nc.psum_tensor("ps_" + name, list(shape), dt))

        OP = sc.op

        def ln_generic(src, skey, s12, s12k, junk, jkey, mv, mvk, rstd, rk):
            OP("dve", lambda e: e.reduce_sum(out=s12[:, 0:1], in_=src[:], axis=AX.X), reads=[skey], writes=[s12k])
            OP("act", lambda e: e.activation(out=junk, in_=src[:], func=AF.Square, accum_out=s12[:, 1:2]),
               reads=[skey], writes=[s12k + "b", jkey])
            OP("dve", lambda e: e.tensor_scalar(out=mv[:, 0:1], in0=s12[:, 0:1], scalar1=1.0 / 1024.0, scalar2=None, op0=ALU.mult),
               reads=[s12k], writes=[mvk])
            OP("dve", lambda e: e.tensor_tensor(out=s12[:, 2:3], in0=mv[:, 0:1], in1=mv[:, 0:1], op=ALU.mult),
               reads=[mvk], writes=[s12k + "c"])
            OP("dve", lambda e: e.scalar_tensor_tensor(out=mv[:, 1:2], in0=s12[:, 1:2], scalar=1.0 / 1024.0, in1=s12[:, 2:3],
                                                       op0=ALU.mult, op1=ALU.subtract), reads=[s12k + "b", s12k + "c"], writes=[mvk])
            rsqrt_eps(rstd[:], mv[:, 1:2], 1.0, mvk, rk)

        def ln_gen(src, skey, s12, s12k, junk, jkey, mv, mvk, rstd, rk):
            OP("dve", lambda e: e.reduce_sum(out=s12[:, 0:1], in_=src[:], axis=AX.X), reads=[skey], writes=[s12k])
            OP("act", lambda e: e.activation(out=junk, in_=src[:], func=AF.Square, accum_out=s12[:, 1:2]),
               reads=[skey], writes=[s12k + "b", jkey])
            yield
            OP("dve", lambda e: e.tensor_scalar(out=mv[:, 0:1], in0=s12[:, 0:1], scalar1=1.0 / 1024.0, scalar2=None, op0=ALU.mult),
               reads=[s12k], writes=[mvk])
            OP("dve", lambda e: e.tensor_tensor(out=s12[:, 2:3], in0=mv[:, 0:1], in1=mv[:, 0:1], op=ALU.mult),
               reads=[mvk], writes=[s12k + "c"])
            yield
            OP("dve", lambda e: e.scalar_tensor_tensor(out=mv[:, 1:2], in0=s12[:, 1:2], scalar=1.0 / 1024.0, in1=s12[:, 2:3],
                                                       op0=ALU.mult, op1=ALU.subtract), reads=[s12k + "b", s12k + "c"], writes=[mvk])
            OP("dve", lambda e: e.tensor_scalar(out=rstd[:], in0=mv[:, 1:2], scalar1=1.0, scalar2=EPS, op0=ALU.mult, op1=ALU.add),
               reads=[mvk], writes=[rk])
            yield
            OP("act", lambda e: e.activation(out=rstd[:], in_=rstd[:], func=AF.Sqrt), reads=[rk], writes=[rk])
            yield
            OP("dve", lambda e: e.reciprocal(out=rstd[:], in_=rstd[:]), reads=[rk], writes=[rk])

        def rsqrt_eps(dst, src, scale, skey, dkey):
            OP("dve", lambda e: e.tensor_scalar(out=dst, in0=src, scalar1=scale, scalar2=EPS, op0=ALU.mult, op1=ALU.add),
               reads=[skey], writes=[dkey])
            OP("act", lambda e: e.activation(out=dst, in_=dst, func=AF.Sqrt), reads=[dkey], writes=[dkey])
            OP("dve", lambda e: e.reciprocal(out=dst, in_=dst), reads=[dkey], writes=[dkey])

        identf = sbt(top, "identf", [128, 128], F32)
        identb = sbt(top, "identb", [128, 128], BF16)
        g0T = sbt(top, "g0T", [128, 8], F32)
        b0T = sbt(top, "b0T", [128, 8], F32)
        pers = ExitStack()
        pers.__enter__()
        KsT = sbt(pers, "KsT", [128, S], BF16)
        KwT = sbt(pers, "KwT", [128, S], BF16)
        kcT = sbt(pers, "kcT", [128, NCT * 128], BF16)
        vcA = sbt(pers, "vcA", [128, NCT, 2, 65], BF16)

        sc.dma("sp", identf[:], ident, writes=["identf"], slot="c_identf")
        sc.dma("sp", g0T[:], ln0gT, writes=["g0T"], slot="c_g0T")
        sc.dma("sp", b0T[:], ln0bT, writes=["b0T"], slot="c_b0T")
        OP("dve", lambda e: e.tensor_copy(out=identb[:], in_=identf[:]), reads=["identf"], writes=["identb"])
        OP("dve", lambda e: e.memset(vcA[:], 1.0), writes=["vcA"])
        OP("dve", lambda e: e.memset(kcT[:], 0.0), writes=["kcT"])

        with ExitStack() as p1:
            Wb = sbt(p1, "Wb", [128, 8, DIN], BF16)
            with ExitStack() as pw:
                wst = sbt(pw, "wst", [128, DIN], F32)
                for c in range(8):
                    sc.dma("sp", wst[:], w_in[c * 128:(c + 1) * 128, :], writes=["wst"], slot="wst")
                    eng = ("dve", "pool")[c % 2]
                    OP(eng, lambda e, c=c: e.tensor_copy(out=Wb[:, c, :], in_=wst[:]), reads=["wst"], writes=["Wb"])
                sc.barrier()
            xbuf = [sbt(p1, f"xb{i}", [128, D], F32) for i in range(2)]
            xn = sbt(p1, "xn", [128, D], F32)
            stats = sbt(p1, "stats", [128, 12], F32)
            mv = sbt(p1, "mv", [128, 2], F32)
            rstd = sbt(p1, "rstd", [128, 1], F32)
            tmpf = sbt(p1, "tmpf", [128, 8, 128], F32)
            hT2 = [sbt(p1, f"hT{k}", [128, 8, 512], BF16) for k in range(2)]
            qst = sbt(p1, "qst", [128, 4, 512], BF16)
            cst = sbt(p1, "cst", [128, 2, 2, 512], BF16)
            posTs = sbt(p1, "posTs", [128, 32], F32)
            qgS2 = [sbt(p1, f"qgS{k}", [128, 2, 512], F32) for k in range(2)]
            kgS2 = [sbt(p1, f"kgS{k}", [128, 2, 512], F32) for k in range(2)]
            alT2 = [sbt(p1, f"alT{k}", [16, 512], BF16) for k in range(2)]
            vst = sbt(p1, "vst", [128, 2, 2, 65], BF16)
            OP("dve", lambda e: e.memset(vst[:], 1.0), writes=["vst"])
            gst = sbt(p1, "gst", [128, 24], F32)
            znst = sbt(p1, "znst", [128, 512], F32)
            vgS2 = [sbt(p1, f"vgS{k}", [128, 4, 512], BF16) for k in range(2)]
            zgS2 = [sbt(p1, f"zgS{k}", [128, 4, 512], F32) for k in range(2)]
            wa2f = sbt(p1, "wa2f", [16, 256], F32)
            wa2b = sbt(p1, "wa2b", [16, 256], BF16)
            nba = sbt(p1, "nba", [128, 2], F32)
            gnwB = sbt(p1, "gnwB", [128, 128], F32)
            seg01 = sbt(p1, "seg01", [128, 512], F32)
            trif = sbt(p1, "trif", [128, 128], F32)
            e1 = sbt(p1, "e1", [128, 512], F32)
            spb = sbt(p1, "spb", [128, 512], F32)
            cum = sbt(p1, "cum", [128, 512], F32)
            Eb = sbt(p1, "Eb", [128, 512], F32)
            Ei = sbt(p1, "Ei", [128, 512], F32)
            dl = e1
            EL = spb
            dec = sbt(p1, "dec", [128, 2, 8], F32)
            qeZ = [[sbt(p1, f"qeZ{hp}{h2}", [128, 512], BF16) for h2 in range(2)] for hp in range(2)]
            keT = sbt(p1, "keT", [128, 2, 512], BF16)
            klT = sbt(p1, "klT", [128, 2, 512], BF16)
            klZ4 = [[sbt(p1, f"klZ{hp}{ch}", [128, 128], BF16) for ch in range(2)] for hp in range(2)]
            Am4 = [sbt(p1, f"Am{h}", [128, 128], BF16) for h in range(4)]
            Sf = [sbt(p1, f"Sf{hp}", [128, 128], F32) for hp in range(2)]
            Sb = [[sbt(p1, f"Sb{hp}{i}", [128, 128], BF16) for i in range(3)] for hp in range(2)]
            oS = spb
            sq = e1
            ss = sbt(p1, "ss", [128, 4], F32)
            mixg = sbt(p1, "mixg", [128, 512], BF16)

            tp = pst(p1, "tp", [128, 8, 128], F32)
            psA = [pst(p1, f"psA{i}", [128, 512], F32) for i in range(2)]
            ops = pst(p1, "ops", [128, 512], F32)
            aps = pst(p1, "aps", [128, 4, 128], F32)
            kvps = pst(p1, "kvps", [128, 2, 256], F32)
            tpb2 = pst(p1, "tpb2", [128, 2, 128], BF16)

            for nm, src, dst in (("posTs", posT, posTs), ("wa2f", w_a2, wa2f), ("nba", nbaT, nba),
                                 ("seg01", seg01_d, seg01), ("trif", tri_d, trif)):
                sc.dma("sp", dst[:], src, writes=[nm], slot="c_" + nm)
            sc.dma("sp", gnwB[:], gnw_r.partition_broadcast(128), writes=["gnwB"], slot="c_gnwB")
            OP("dve", lambda e: e.tensor_copy(out=wa2b[:], in_=wa2f[:]), reads=["wa2f"], writes=["wa2b"])
            OP("dve", lambda e: e.tensor_scalar(out=nba[:], in0=nba[:], scalar1=-1.0, scalar2=None, op0=ALU.mult),
               reads=["nba"], writes=["nba"])
            for hp in range(2):
                OP("dve", lambda e, hp=hp: e.memset(Sf[hp][:], 0.0), writes=[f"Sf{hp}"])
                OP("dve", lambda e, hp=hp: e.memset(Sb[hp][0][:], 0.0), writes=[f"Sb{hp}0"])
                for h2 in range(2):
                    OP("dve", lambda e, hp=hp, h2=h2: e.memset(qeZ[hp][h2][:], 0.0), writes=[f"qeZ{hp}{h2}"])
            for hp in range(2):
                for ch in range(2):
                    OP("dve", lambda e, ch=ch, hp=hp: e.memset(klZ4[hp][ch][:], 0.0), writes=[f"klZ{hp}{ch}"])
            sbi = [0, 0]
            pk = [0]

            def nextps():
                pk[0] += 1
                return psA[pk[0] % 2], f"psA{pk[0] % 2}"

            s12 = sbt(p1, "s12", [128, 4], F32)

            def ln_tile(es_x, xb, xbk, mvk="mv"):
                ln_generic(xb, xbk, s12, "s12", tmpf[:].rearrange("p a b -> p (a b)"), "tmpf", mv, "mv", rstd, "rstd")

            def lnG(G):
                hT = hT2[G % 2]; hk = f"hT{G % 2}"
                for tt in range(4):
                    t = 4 * G + tt
                    xb = xbuf[t % 2]; xbk = f"xb{t % 2}"
                    sc.dma("sp", xb[:], x[t * 128:(t + 1) * 128, :], writes=[xbk], slot=xbk)
                    yield
                    yield from ln_gen(xb, xbk, s12, "s12", tmpf[:].rearrange("p a b -> p (a b)"), "tmpf", mv, "mv", rstd, "rstd")
                    yield
                    OP("dve", lambda e: e.tensor_scalar(out=xn[:], in0=xb[:], scalar1=mv[:, 0:1], scalar2=rstd[:, 0:1],
                                                        op0=ALU.subtract, op1=ALU.mult),
                       reads=[xbk, "mv", "rstd"], writes=["xn"])
                    yield
                    for c in range(8):
                        OP("pe", lambda e, c=c: e.transpose(tp[:, c, :], xn[:, c * 128:(c + 1) * 128], identf[:]),
                           reads=["xn", "identf"], writes=["tp"])
                    yield
                    for hf in range(2):
                        OP("dve", lambda e, hf=hf: e.tensor_tensor(
                            out=tmpf[:, 4 * hf:4 * hf + 4, :], in0=tp[:, 4 * hf:4 * hf + 4, :],
                            in1=g0T[:, 4 * hf:4 * hf + 4].unsqueeze(2).to_broadcast([128, 4, 128]), op=ALU.mult),
                           reads=["tp", "g0T"], writes=["tmpf"])
                    yield
                    OP("pool", lambda e, tt=tt: e.tensor_tensor(
                        out=hT[:, :, tt * 128:(tt + 1) * 128], in0=tmpf[:],
                        in1=b0T[:].unsqueeze(2).to_broadcast([128, 8, 128]), op=ALU.add),
                       reads=["tmpf", "b0T"], writes=[hk])
                    yield

            def frontG(G):
                hT = hT2[G % 2]; hk = f"hT{G % 2}"
                qgS = qgS2[G % 2]; kgS = kgS2[G % 2]; alT = alT2[G % 2]; vgS = vgS2[G % 2]; zgS = zgS2[G % 2]
                kq = f"qgS{G % 2}"; kk = f"kgS{G % 2}"; ka = f"alT{G % 2}"; kv = f"vgS{G % 2}"; kz = f"zgS{G % 2}"

                def proj_fm(col0, M):
                    ps, pkey = nextps()
                    for c in range(8):
                        OP("pe", lambda e, c=c: e.matmul(ps[0:M, :], lhsT=Wb[:, c, col0:col0 + M], rhs=hT[:, c, :],
                                                         start=(c == 0), stop=(c == 7)),
                           reads=["Wb", hk], writes=[pkey])
                        if c == 3:
                            yield
                    return ps, pkey

                yield
                tsl = slice(G * 512, (G + 1) * 512)
                for r in range(4):
                    ps, pkey = yield from proj_fm(C_Q + 128 * r, 128)
                    OP("act", lambda e, r=r, ps=ps: e.copy(out=qst[:, r, :], in_=ps[:]), reads=[pkey], writes=["qst"])
                sc.dma("sp", qT_s[:, :, tsl], qst[:], reads=["qst"], writes=["qT_s"], slot="qst", multi=True)
                ps, pkey = yield from proj_fm(C_KS, 128)
                OP("act", lambda e, ps=ps: e.copy(out=KsT[:, tsl], in_=ps[:]), reads=[pkey], writes=["KsT"])
                ps, pkey = yield from proj_fm(C_KW, 128)
                OP("act", lambda e, ps=ps: e.copy(out=KwT[:, tsl], in_=ps[:]), reads=[pkey], writes=["KwT"])
                for net, col in ((0, C_KC), (1, C_VC)):
                    ps, pkey = yield from proj_fm(col, 128)
                    for ab in range(2):
                        OP("dve", lambda e, ps=ps, net=net, ab=ab: e.tensor_tensor(
                            out=cst[:, net, ab, :].rearrange("p (j l) -> p j l", l=16),
                            in0=ps[:].rearrange("p (j l) -> p j l", l=16),
                            in1=posTs[:, 16 * ab:16 * ab + 16].unsqueeze(1).to_broadcast([128, 32, 16]), op=ALU.add),
                           reads=[pkey, "posTs"], writes=["cst"])
                for net in range(2):
                    for ab in range(2):
                        sc.dma("sp", cab_s[net, ab, :, tsl], cst[:, net, ab, :], reads=["cst"], writes=["cab_s"],
                               slot="cst", multi=True)
                for hp in range(2):
                    ps, pkey = yield from proj_fm(C_QG + 128 * hp, 128)
                    OP("act", lambda e, ps=ps, hp=hp: e.copy(out=qgS[:, hp, :], in_=ps[:]), reads=[pkey], writes=[kq])
                    ps, pkey = yield from proj_fm(C_KG + 128 * hp, 128)
                    OP("act", lambda e, ps=ps, hp=hp: e.copy(out=kgS[:, hp, :], in_=ps[:]), reads=[pkey], writes=[kk])
                ps, pkey = yield from proj_fm(C_AL, 16)
                OP("act", lambda e, ps=ps: e.copy(out=alT[:], in_=ps[0:16, :]), reads=[pkey], writes=[ka])

                for tt in range(4):
                    t = 4 * G + tt

                    def proj_tm(col0, N):
                        ps, pkey = nextps()
                        for c in range(8):
                            OP("pe", lambda e, c=c: e.matmul(ps[:, 0:N], lhsT=hT[:, c, tt * 128:(tt + 1) * 128],
                                                             rhs=Wb[:, c, col0:col0 + N], start=(c == 0), stop=(c == 7)),
                               reads=["Wb", hk], writes=[pkey])
                            if c == 3:
                                yield
                        return ps, pkey

                    ps, pkey = yield from proj_tm(C_TMA, 280)
                    OP("act", lambda e, ps=ps: e.copy(out=vst[:, :, :, 0:64].rearrange("p w g d -> p (w g) d"),
                                                     in_=ps[:, 0:256].rearrange("p (a d) -> p a d", d=64)),
                       reads=[pkey], writes=["vst"])
                    sc.dma("sp", vsw_s[:, :, t, :], vst[:].rearrange("p w g d -> p w (g d)"), reads=["vst"], writes=["vsw_s"],
                           slot="vst", multi=True)
                    OP("act", lambda e, ps=ps: e.activation(out=gst[:], in_=ps[:, 256:280], func=AF.Tanh, scale=0.5),
                       reads=[pkey], writes=["gst"])
                    OP("dve", lambda e: e.tensor_scalar(out=gst[:], in0=gst[:], scalar1=0.5, scalar2=0.5, op0=ALU.mult, op1=ALU.add),
                       reads=["gst"], writes=["gst"])
                    sc.dma("sp", gates_s[t * 128:(t + 1) * 128, :], gst[:], reads=["gst"], writes=["gates_s"],
                           slot="gst", multi=True)
                    ps, pkey = yield from proj_tm(C_ZN, 512)
                    OP("act", lambda e, ps=ps: e.activation(out=znst[:], in_=ps[:], func=AF.Silu),
                       reads=[pkey], writes=["znst"])
                    sc.dma("sp", zn_s[t * 128:(t + 1) * 128, :], znst[:], reads=["znst"], writes=["zn_s"],
                           slot="znst", multi=True)
                    ps, pkey = yield from proj_tm(C_VG, 512)
                    OP("act", lambda e, ps=ps, tt=tt: e.copy(out=vgS[:, tt, :], in_=ps[:]), reads=[pkey], writes=[kv])
                    ps, pkey = yield from proj_tm(C_ZG, 512)
                    OP("act", lambda e, ps=ps, tt=tt: e.activation(out=zgS[:, tt, :], in_=ps[:], func=AF.Silu),
                       reads=[pkey], writes=[kz])
                    OP("pool", lambda e, tt=tt: e.tensor_tensor(
                        out=zgS[:, tt, :].rearrange("p (h v) -> p h v", h=4),
                        in0=zgS[:, tt, :].rearrange("p (h v) -> p h v", h=4),
                        in1=gnwB[:].unsqueeze(1).to_broadcast([128, 4, 128]), op=ALU.mult),
                       reads=[kz, "gnwB"], writes=[kz])

            def glaG(G):
                qgS = qgS2[G % 2]; kgS = kgS2[G % 2]; alT = alT2[G % 2]; vgS = vgS2[G % 2]; zgS = zgS2[G % 2]
                kq = f"qgS{G % 2}"; kk = f"kgS{G % 2}"; ka = f"alT{G % 2}"; kv = f"vgS{G % 2}"; kz = f"zgS{G % 2}"
                for hp in range(2):
                    yield
                    ps, pkey = nextps()
                    OP("pe", lambda e, ps=ps, hp=hp: e.matmul(ps[:], lhsT=wa2b[0:16, hp * 128:(hp + 1) * 128],
                                                             rhs=alT[0:16, :], start=True, stop=True),
                       reads=["wa2b", ka], writes=[pkey])
                    OP("act", lambda e, ps=ps, hp=hp: e.activation(out=e1[:], in_=ps[:], func=AF.Exp,
                                                                  bias=nba[:, hp:hp + 1], scale=-1.0),
                       reads=[pkey, "nba"], writes=["e1"])
                    yield
                    OP("act", lambda e: e.activation(out=spb[:], in_=e1[:], func=AF.Ln, bias=1.0, scale=1.0),
                       reads=["e1"], writes=["spb"])
                    yield
                    OP("dve", lambda e, hp=hp: e.tensor_tensor_scan(out=cum[:], data0=seg01[:], data1=spb[:],
                                                                   initial=0.0, op0=ALU.mult, op1=ALU.add),
                       reads=["seg01", "spb"], writes=["cum"])
                    yield
                    OP("act", lambda e, hp=hp: e.activation(out=Eb[:], in_=cum[:], func=AF.Exp,
                                                           scale=-1.0 / 16.0), reads=["cum"], writes=["Eb"])
                    OP("act", lambda e, hp=hp: e.activation(out=Ei[:], in_=cum[:], func=AF.Exp,
                                                           scale=1.0 / 16.0), reads=["cum"], writes=["Ei"])
                    yield
                    OP("dve", lambda e, hp=hp: e.tensor_tensor(
                        out=dl[:].rearrange("p (c s) -> p c s", s=64),
                        in0=cum[:].rearrange("p (c s) -> p c s", s=64),
                        in1=cum[:].rearrange("p (c s) -> p c s", s=64)[:, :, 63:64].to_broadcast([128, 8, 64]),
                        op=ALU.subtract), reads=["cum"], writes=["e1"])
                    OP("act", lambda e: e.activation(out=EL[:], in_=dl[:], func=AF.Exp, scale=1.0 / 16.0),
                       reads=["e1"], writes=["spb"])
                    OP("dve", lambda e, hp=hp: e.tensor_copy(
                        out=dec[:, hp, :], in_=Eb[:].rearrange("p (c s) -> p c s", s=64)[:, :, 63]),
                       reads=["Eb"], writes=["dec"])
                    yield
                    for h2 in range(2):
                        rs = slice(64 * h2, 64 * h2 + 64)
                        OP("dve", lambda e, hp=hp, h2=h2, rs=rs: e.scalar_tensor_tensor(
                            out=qeZ[hp][h2][rs, :], in0=qgS[rs, hp, :], scalar=0.125, in1=Eb[rs, :],
                            op0=ALU.mult, op1=ALU.mult), reads=[kq, "Eb"], writes=[f"qeZ{hp}{h2}"])
                    OP("pool", lambda e, hp=hp: e.tensor_tensor(out=keT[:, hp, :], in0=kgS[:, hp, :], in1=Ei[:], op=ALU.mult),
                       reads=[kk, "Ei"], writes=["keT"])
                    OP("pool", lambda e, hp=hp: e.tensor_tensor(out=klT[:, hp, :], in0=kgS[:, hp, :], in1=EL[:], op=ALU.mult),
                       reads=[kk, "spb"], writes=["klT"])
                for tt in range(4):
                    t = 4 * G + tt
                    tk = slice(tt * 128, (tt + 1) * 128)
                    yield
                    for hp in range(2):
                        OP("pe", lambda e, hp=hp: e.transpose(tpb2[:, hp, :], klT[:, hp, tk], identb[:]),
                           reads=["klT", "identb"], writes=["tpb2"])
                    for hp in range(2):
                        for ch in range(2):
                            rs = slice(64 * ch, 64 * ch + 64)
                            OP("act", lambda e, ch=ch, rs=rs, hp=hp: e.copy(out=klZ4[hp][ch][rs, :], in_=tpb2[rs, hp, :]),
                               reads=["tpb2"], writes=[f"klZ{hp}{ch}"])
                    yield
                    sas = [sbi[0], sbi[1]]
                    for hp in range(2):
                        sa = sas[hp]
                        for ch in range(2):
                            OP("pe", lambda e, ch=ch, hp=hp: e.matmul(
                                kvps[:, ch, :], lhsT=klZ4[hp][ch][:], rhs=vgS[:, tt, hp * 256:(hp + 1) * 256],
                                start=True, stop=True), reads=[f"klZ{hp}{ch}", kv], writes=["kvps"])
                        yield
                        for ch in range(2):
                            cidx = 2 * tt + ch
                            for h2 in range(2):
                                rs = slice(64 * h2, 64 * h2 + 64)
                                OP("dve", lambda e, hp=hp, h2=h2, rs=rs, cidx=cidx, ch=ch: e.scalar_tensor_tensor(
                                    out=Sf[hp][rs, :], in0=Sf[hp][rs, :], scalar=dec[rs, hp, cidx:cidx + 1],
                                    in1=kvps[rs, ch, 128 * h2:128 * h2 + 128], op0=ALU.mult, op1=ALU.add),
                                   reads=[f"Sf{hp}", "dec", "kvps"], writes=[f"Sf{hp}"])
                            nb = (sa + 1 + ch) % 3
                            OP("act", lambda e, hp=hp, nb=nb: e.copy(out=Sb[hp][nb][:], in_=Sf[hp][:]),
                               reads=[f"Sf{hp}"], writes=[f"Sb{hp}{nb}"])
                        yield
                    for hp in range(2):
                        for h2 in range(2):
                            h = 2 * hp + h2
                            OP("pe", lambda e, hp=hp, h2=h2, h=h: e.matmul(aps[:, h, :], lhsT=keT[:, hp, tk],
                                                                           rhs=qeZ[hp][h2][:, tk], start=True, stop=True),
                               reads=["keT", f"qeZ{hp}{h2}"], writes=["aps"])
                    yield
                    for h in range(4):
                        OP("dve", lambda e, h=h: e.tensor_tensor(out=Am4[h][:], in0=aps[:, h, :], in1=trif[:], op=ALU.mult),
                           reads=["aps", "trif"], writes=[f"Am{h}"])
                    yield
                    first = True
                    for hp in range(2):
                        sa = sas[hp]
                        for h2 in range(2):
                            h = 2 * hp + h2
                            OP("pe", lambda e, h=h, first=first: e.matmul(
                                ops[:, h * 128:(h + 1) * 128], lhsT=Am4[h][:], rhs=vgS[:, tt, h * 128:(h + 1) * 128],
                                start=first, stop=False, skip_group_check=True), reads=[f"Am{h}", kv], writes=["ops"])
                            first = False
                            for ch in range(2):
                                sbk = (sa + ch) % 3
                                OP("pe", lambda e, h=h, hp=hp, h2=h2, ch=ch, sbk=sbk: e.matmul(
                                    ops[64 * ch:64 * ch + 64, h * 128:(h + 1) * 128],
                                    lhsT=qeZ[hp][h2][:, tt * 128 + 64 * ch:tt * 128 + 64 * ch + 64],
                                    rhs=Sb[hp][sbk][:], start=False, stop=(ch == 1), skip_group_check=True),
                                   reads=[f"qeZ{hp}{h2}", f"Sb{hp}{sbk}"], writes=["ops"])
                        sbi[hp] = (sa + 2) % 3
                    yield
                    OP("act", lambda e: e.copy(out=oS[:], in_=ops[:]), reads=["ops"], writes=["spb"])
                    yield
                    OP("pool", lambda e: e.tensor_tensor(out=sq[:], in0=oS[:], in1=oS[:], op=ALU.mult),
                       reads=["spb"], writes=["e1"])
                    OP("dve", lambda e: e.reduce_sum(out=ss[:], in_=sq[:].rearrange("p (h v) -> p h v", h=4), axis=AX.X),
                       reads=["e1"], writes=["ss"])
                    rsqrt_eps(ss[:], ss[:], 1.0 / 128.0, "ss", "ss")
                    OP("pool", lambda e: e.tensor_tensor(
                        out=sq[:].rearrange("p (h v) -> p h v", h=4), in0=oS[:].rearrange("p (h v) -> p h v", h=4),
                        in1=ss[:].unsqueeze(2).to_broadcast([128, 4, 128]), op=ALU.mult),
                       reads=["spb", "ss"], writes=["e1"])
                    OP("pool", lambda e, tt=tt: e.tensor_tensor(out=mixg[:], in0=sq[:], in1=zgS[:, tt, :], op=ALU.mult),
                       reads=["e1", kz], writes=["mixg"])
                    sc.dma("sp", mixg_s[t * 128:(t + 1) * 128, :], mixg[:], reads=["mixg"], writes=["mixg_s"],
                           slot="mixg", multi=True)

            def zip_many(gens):
                act_ = [g for g in gens if g is not None]
                while act_:
                    for g in list(act_):
                        try:
                            next(g)
                        except StopIteration:
                            act_.remove(g)

            zip_many([lnG(0)])
            for G in range(NG):
                zip_many([frontG(G), lnG(G + 1) if G + 1 < NG else None, glaG(G - 1) if G >= 1 else None])
            zip_many([glaG(NG - 1)])
            sc.barrier()

        pre2 = ExitStack()
        pre2.__enter__()
        Vs = sbt(pre2, "Vs", [128, NT, 2, 65], BF16)
        Vw = sbt(pre2, "Vw", [128, NT, 2, 65], BF16)
        sc.dma("sp", Vs[:].rearrange("p t g d -> p (t g d)"), vsw_s[:, 0, :, :].rearrange("p t c -> p (t c)"),
               reads=["vsw_s"], writes=["Vs"], slot="Vs")
        sc.dma("sp", Vw[:].rearrange("p t g d -> p (t g d)"), vsw_s[:, 1, :, :].rearrange("p t c -> p (t c)"),
               reads=["vsw_s"], writes=["Vw"], slot="Vw")
        fst = sbt(pre2, "fst", [128, 2048], F32)
        Fb = sbt(pre2, "Fb", [128, S], BF16)
        for c in range((S + 2047) // 2048):
            w = min(2048, S - c * 2048)
            sc.dma("sp", fst[:, 0:w], Ffull_d[:, c * 2048:c * 2048 + w], writes=["fst"], slot="fst")
            OP("dve", lambda e, c=c, w=w: e.tensor_copy(out=Fb[:, c * 2048:c * 2048 + w], in_=fst[:, 0:w]),
               reads=["fst"], writes=["Fb"])
        with ExitStack() as p1b:
            cA = sbt(p1b, "cA", [128, S], BF16)
            cB = sbt(p1b, "cB", [128, S], BF16)
            imc = sbt(p1b, "imc", [128, 32, S // 16], BF16)
            w1st2 = [sbt(p1b, f"w1st{k}", [128, 8, 256], F32) for k in range(2)]
            w1b = sbt(p1b, "w1b", [128, 32, 256], BF16)
            w2st = sbt(p1b, "w2st", [128, 2, 64], F32)
            w2b = sbt(p1b, "w2b", [128, 2, 64], BF16)
            b1T = sbt(p1b, "b1T", [128, 2], F32)
            xg = sbt(p1b, "xg", [128, 512], F32)
            ug = sbt(p1b, "ug", [128, 512], F32)
            sg = sbt(p1b, "sg", [128, 512], F32)
            h1T = sbt(p1b, "h1T", [128, 2, 512], BF16)
            hps = [pst(p1b, f"hps{i}", [128, 512], F32) for i in range(4)]
            ps2 = pst(p1b, "ps2", [128, 512], F32)
            ps3 = pst(p1b, "ps3", [128, 64], F32)
            OP("dve", lambda e: e.memset(h1T[:], 0.0), writes=["h1T"])
            for net in range(2):
                sc.dma("sp", cA[:], cab_s[net, 0], reads=["cab_s"], writes=["cA"], slot="cA")
                sc.dma("sp", cB[:], cab_s[net, 1], reads=["cab_s"], writes=["cB"], slot="cB")
                w1v = w1_d[net].rearrange("(l d) h -> d l h", d=64)
                for lc in range(4):
                    w1st = w1st2[lc % 2]; wk_ = f"w1st{lc % 2}"
                    for half in range(2):
                        sc.dma("sp", w1st[64 * half:64 * half + 64, :, :], w1v[:, 8 * lc:8 * lc + 8, :],
                               writes=[wk_], slot=wk_)
                    OP(("dve", "pool")[lc % 2], lambda e, lc=lc, w1st=w1st: e.tensor_copy(out=w1b[:, 8 * lc:8 * lc + 8, :], in_=w1st[:]),
                       reads=[wk_], writes=["w1b"])
                sc.dma("sp", w2st[:], w2_d[net].rearrange("(c p) d -> p c d", p=128), writes=["w2st"], slot="w2st")
                OP("dve", lambda e: e.tensor_copy(out=w2b[:], in_=w2st[:]), reads=["w2st"], writes=["w2b"])
                sc.dma("sp", b1T[:], b1T_d[net], writes=["b1T"], slot="b1T")
                cAv = cA[:].rearrange("p (j l) -> p j l", l=16)
                cBv = cB[:].rearrange("p (j l) -> p j l", l=16)
                for l in range(32):
                    src_ = cAv[:, 0:NCB, l] if l < 16 else cBv[:, 1:NCB + 1, l - 16]
                    OP(("dve", "dve", "pool")[l % 3], lambda e, l=l, src_=src_: e.tensor_copy(out=imc[:, l, 0:NCB], in_=src_),
                       reads=["cA", "cB"], writes=[f"imc{l}"])
                for g in range(2):
                    rs = slice(64 * g, 64 * g + 64)
                    for hc in range(2):
                        ps = hps[2 * g + hc]; pkey = f"hps{2 * g + hc}"
                        for l in range(32):
                            rhs = imc[rs, l, 0:NCB]
                            OP("pe", lambda e, ps=ps, l=l, rhs=rhs, hc=hc, rs=rs: e.matmul(
                                ps[:, 0:NCB], lhsT=w1b[rs, l, hc * 128:(hc + 1) * 128], rhs=rhs,
                                start=(l == 0), stop=(l == 31)), reads=["w1b", f"imc{l}"], writes=[pkey])
                        OP("act", lambda e, ps=ps, hc=hc: e.activation(out=xg[:, 0:NCB], in_=ps[:, 0:NCB], func=AF.Identity,
                                                                      bias=b1T[:, hc:hc + 1], scale=1.0),
                           reads=[pkey, "b1T"], writes=["xg"])
                        OP("pool", lambda e: e.tensor_tensor(out=ug[:, 0:NCB], in0=xg[:, 0:NCB], in1=xg[:, 0:NCB], op=ALU.mult),
                           reads=["xg"], writes=["ug"])
                        OP("dve", lambda e: e.tensor_scalar(out=ug[:, 0:NCB], in0=ug[:, 0:NCB], scalar1=0.044715, scalar2=1.0,
                                                            op0=ALU.mult, op1=ALU.add), reads=["ug"], writes=["ug"])
                        OP("pool", lambda e: e.tensor_tensor(out=ug[:, 0:NCB], in0=ug[:, 0:NCB], in1=xg[:, 0:NCB], op=ALU.mult),
                           reads=["xg", "ug"], writes=["ug"])
                        OP("act", lambda e: e.activation(out=sg[:, 0:NCB], in_=ug[:, 0:NCB], func=AF.Sigmoid,
                                                         scale=2.0 * math.sqrt(2.0 / math.pi)), reads=["ug"], writes=["sg"])
                        OP("pool", lambda e, hc=hc: e.tensor_tensor(out=h1T[:, hc, 0:NCB], in0=xg[:, 0:NCB], in1=sg[:, 0:NCB],
                                                                   op=ALU.mult), reads=["xg", "sg"], writes=["h1T"])
                    if net == 0:
                        for hc in range(2):
                            OP("pe", lambda e, hc=hc, rs=rs: e.matmul(ps2[rs, 0:NCB], lhsT=w2b[:, hc, :], rhs=h1T[:, hc, 0:NCB],
                                                                      start=(hc == 0), stop=(hc == 1)),
                               reads=["w2b", "h1T"], writes=["ps2"])
                        OP("act", lambda e, rs=rs: e.copy(out=kcT[rs, 0:NCB], in_=ps2[rs, 0:NCB]), reads=["ps2"], writes=["kcT"])
                    else:
                        for j in range(NCT):
                            for hc in range(2):
                                OP("pe", lambda e, hc=hc, j=j: e.matmul(ps3[:], lhsT=h1T[:, hc, j * 128:(j + 1) * 128],
                                                                        rhs=w2b[:, hc, :], start=(hc == 0), stop=(hc == 1)),
                                   reads=["w2b", "h1T"], writes=["ps3"])
                            OP("act", lambda e, j=j, g=g: e.copy(out=vcA[:, j, g, 0:64], in_=ps3[:]), reads=["ps3"], writes=["vcA"])
            sc.barrier()

        if dbg:
            for nm, t_, shp in (("KsT", KsT, [128, S]), ("KwT", KwT, [128, S]), ("kcT", kcT, [128, NCT * 128]),
                                ("vcA", vcA, [128, NCT, 2, 65])):
                d_ = nc.dram_tensor("dbg_" + nm, shp, BF16, kind="ExternalOutput").ap()
                sc.dma("sp", d_, t_[:], slot="dbg_" + nm)
            sc.barrier()
        mixn_s = dscr("mixn_s", [S, 512], BF16)
        with ExitStack() as p2:
            bnear = sbt(p2, "bnear", [128, 2, 8, 128], F32)
            for dd in range(2):
                sc.dma("sp", bnear[:, dd, :, :], biasNear[dd], writes=["bnear"], slot="bnear")
            cmpB = [[sbt(p2, f"cmpB{s}{k}", [128, 8, 128], F32) for k in range(2)] for s in range(2)]
            bn8 = sbt(p2, "bn8", [128, 2, 8, 128], F32)
            bnH = sbt(p2, "bnH", [128, 2, 8, 128], BF16)
            bnL = sbt(p2, "bnL", [128, 2, 8, 128], BF16)
            OP("dve", lambda e: e.tensor_scalar(out=bn8[:], in0=bnear[:], scalar1=8.0, scalar2=None, op0=ALU.mult),
               reads=["bnear"], writes=["bn8"])
            OP("dve", lambda e: e.tensor_copy(out=bnH[:], in_=bn8[:]), reads=["bn8"], writes=["bnH"])
            OP("dve", lambda e: e.tensor_tensor(out=bn8[:], in0=bn8[:], in1=bnH[:], op=ALU.subtract),
               reads=["bn8", "bnH"], writes=["bn8"])
            OP("dve", lambda e: e.tensor_copy(out=bnL[:], in_=bn8[:]), reads=["bn8"], writes=["bnL"])
            ovlf = sbt(p2, "ovlf", [128, NCT, 128], F32)
            ovlb = sbt(p2, "ovlb", [128, NCT, 128], BF16)
            sc.dma("sp", ovlf[:], ovl_d, writes=["ovlf"], slot="c_ovlf")
            OP("dve", lambda e: e.tensor_copy(out=ovlb[:], in_=ovlf[:]), reads=["ovlf"], writes=["ovlb"])
            Tm = sbt(p2, "Tm", [128, 256], F32); Ta = sbt(p2, "Ta", [128, 256], F32)
            sc.dma("sp", Tm[:], topk_m, writes=["Tm"], slot="c_Tm")
            sc.dma("sp", Ta[:], topk_a, writes=["Ta"], slot="c_Ta")
            c8b = sbt(p2, "c8b", [128, 8], F32)
            sc.dma("sp", c8b[:], rb31_r.partition_broadcast(128), writes=["c8b"], slot="c_c8b")
            OP("dve", lambda e: e.tensor_scalar(out=c8b[:], in0=c8b[:], scalar1=8.0, scalar2=None, op0=ALU.mult),
               reads=["c8b"], writes=["c8b"])
            onesrow = sbt(p2, "onesrow", [128, 128], BF16)
            OP("dve", lambda e: e.memset(onesrow[:], 0.0), writes=["onesrow"])
            OP("dve", lambda e: e.memset(onesrow[0:1, :], 1.0), writes=["onesrow"])
            c8rhs = [sbt(p2, f"c8rhs{g}", [128, 4, 128], BF16) for g in range(2)]
            B4 = [sbt(p2, f"B4{g}", [128, 4, 128], BF16) for g in range(2)]
            b4f = sbt(p2, "b4f", [128, 128], F32)
            sc.dma("sp", b4f[:], band4, writes=["b4f"], slot="c_b4f")
            for g in range(2):
                OP("dve", lambda e, g=g: e.memset(c8rhs[g][:], 0.0), writes=[f"c8rhs{g}"])
                OP("dve", lambda e, g=g: e.tensor_copy(out=c8rhs[g][0:1, :, :],
                                                       in_=c8b[0:1, 4 * g:4 * g + 4].unsqueeze(2).to_broadcast([1, 4, 128])),
                   reads=["c8b"], writes=[f"c8rhs{g}"])
                OP("dve", lambda e, g=g: e.tensor_tensor(out=B4[g][:], in0=b4f[:].unsqueeze(1).to_broadcast([128, 4, 128]),
                                                         in1=c8b[:, 4 * g:4 * g + 4].unsqueeze(2).to_broadcast([128, 4, 128]),
                                                         op=ALU.add), reads=["b4f", "c8b"], writes=[f"B4{g}"])
            QTz = [[sbt(p2, f"QTz{s}{g}", [128, 4, 128], BF16) for g in range(2)] for s in range(2)]
            for s in range(2):
                for g in range(2):
                    OP("dve", lambda e, s=s, g=g: e.memset(QTz[s][g][:], 0.0), writes=[f"QTz{s}{g}"])
            Pb = [sbt(p2, f"Pb{i}", [128, 512], BF16) for i in range(5)]
            tmpb = [sbt(p2, f"tmpb{i}", [128, 512], F32) for i in range(2)]
            ocn = [sbt(p2, f"ocn{s}", [128, 8, 64], F32) for s in range(2)]
            osn = sbt(p2, "osn", [128, 8, 64], F32)
            own = sbt(p2, "own", [128, 8, 64], F32)
            MnT = [[sbt(p2, f"MnT{s}{g}", [128, 4, 128], BF16) for g in range(2)] for s in range(2)]
            MnR = [[sbt(p2, f"MnR{s}{g}", [128, 4, 128], BF16) for g in range(2)] for s in range(2)]
            rz = sbt(p2, "rz", [128, 4], F32)
            tU = sbt(p2, "tU", [128, 4, 128], F32)
            imp = sbt(p2, "imp", [128, 128], F32)
            t2 = sbt(p2, "t2", [128, 128], F32)
            t3 = sbt(p2, "t3", [128, 128], F32)
            m1 = sbt(p2, "m1", [128, 8], F32)
            m2 = sbt(p2, "m2", [128, 8], F32)
            mq = sbt(p2, "mq", [128, 128], BF16)
            gt = sbt(p2, "gt", [128, 24], F32)
            znt = sbt(p2, "znt", [128, 512], F32)
            acc = sbt(p2, "acc", [128, 8, 64], F32)
            tacc = sbt(p2, "tacc", [128, 8, 64], F32)
            mixn = sbt(p2, "mixn", [128, 512], BF16)

            ST = [pst(p2, f"ST{i}", [128, 512], F32) for i in range(3)]
            YY = pst(p2, "YY", [128, 2, 512], F32)
            OSp = pst(p2, "OSp", [128, 512], F32)
            OWp = pst(p2, "OWp", [128, 512], F32)
            TT = pst(p2, "TT", [128, 128], BF16)
            rot = {"st": 0, "p": 0, "t": 0}

            def nxt(kind, n):
                rot[kind] = (rot[kind] + 1) % n
                return rot[kind]

            def s_tile(lhsT, lkeys, sl, g, second, near_bias):
                si = nxt("st", 3); st = ST[si]; stk = f"ST{si}"
                qk = f"QTz{sl}{g}"
                extras = [] if second is None else (second if isinstance(second, list) else [second])
                OP("pe", lambda e: e.matmul(st[:], lhsT=lhsT, rhs=QTz[sl][g][:].rearrange("p r q -> p (r q)"),
                                            start=True, stop=(not extras)), reads=lkeys + [qk], writes=[stk])
                for xi, (l2, r2, k2) in enumerate(extras):
                    OP("pe", lambda e, l2=l2, r2=r2, xi=xi: e.matmul(st[:], lhsT=l2, rhs=r2, start=False,
                                                                     stop=(xi == len(extras) - 1)), reads=k2, writes=[stk])
                pi = nxt("p", 5); P = Pb[pi]; pk_ = f"Pb{pi}"
                if near_bias is None:
                    OP("act", lambda e: e.activation(out=P[:], in_=st[:], func=AF.Exp, scale=0.125), reads=[stk], writes=[pk_])
                else:
                    bap, bkey = near_bias
                    ti = nxt("t", 2); tb = tmpb[ti]; tk_ = f"tmpb{ti}"
                    OP("dve", lambda e: e.scalar_tensor_tensor(out=tb[:].rearrange("p (r q) -> p r q", r=4),
                                                               in0=st[:].rearrange("p (r q) -> p r q", r=4), scalar=0.125,
                                                               in1=bap, op0=ALU.mult, op1=ALU.add),
                       reads=[stk, bkey], writes=[tk_])
                    OP("act", lambda e: e.activation(out=P[:], in_=tb[:], func=AF.Exp), reads=[tk_], writes=[pk_])
                return P, pk_

            def pv(P, pk_, acc_ps, akey, width, rhs, rkeys, first):
                for r in range(4):
                    OP("pe", lambda e, r=r: e.matmul(acc_ps[:, r * width:(r + 1) * width], lhsT=P[:, r * 128:(r + 1) * 128],
                                                     rhs=rhs, start=(first and r == 0), stop=False, skip_group_check=True),
                       reads=[pk_] + rkeys, writes=[akey])

            pend_pv = []
            pend_late = []

            def tick():
                for q_ in (pend_pv, pend_late):
                    for it in q_:
                        it[0] -= 1
                    while q_ and q_[0][0] <= 0:
                        q_.pop(0)[1]()

            def flush_all():
                while pend_pv or pend_late:
                    for q_ in (pend_pv, pend_late):
                        while q_:
                            q_.pop(0)[1]()

            def unit(s_args, pv_list):
                P, pk_ = s_tile(*s_args)
                tick()
                pend_pv.append([2, lambda: [pv(P, pk_, *a_) for a_ in pv_list]])

            mqs = [[sbt(p2, f"mq{s_}{g}", [128, 128], BF16) for g in range(2)] for s_ in range(2)]
            gts = [sbt(p2, f"gt{s_}", [128, 24], F32) for s_ in range(2)]
            znts = [sbt(p2, f"znt{s_}", [128, 512], F32) for s_ in range(2)]

            def finalizeA(i, g, sl):
                OC = YY[:, 0, 0:260]; U = YY[:, 1, :]
                OCv = OC.rearrange("p (r d) -> p r d", r=4)
                mqk = f"mq{sl}{g}"; mqb = mqs[sl][g]
                OP("dve", lambda e: e.tensor_scalar(out=rz[:], in0=OCv[:, :, 64], scalar1=1e-30, scalar2=None, op0=ALU.max),
                   reads=["YY0"], writes=["rz"])
                OP("dve", lambda e: e.reciprocal(out=rz[:], in_=rz[:]), reads=["rz"], writes=["rz"])
                OP("dve", lambda e: e.tensor_tensor(out=ocn[sl][:, 4 * g:4 * g + 4, :], in0=OCv[:, :, 0:64],
                                                    in1=rz[:].unsqueeze(2).to_broadcast([128, 4, 64]), op=ALU.mult),
                   reads=["YY0", "rz"], writes=[f"ocn{sl}"])
                OP("dve", lambda e: e.tensor_tensor(out=tU[:], in0=U.rearrange("p (r n) -> p r n", r=4),
                                                    in1=rz[:].unsqueeze(2).to_broadcast([128, 4, 128]), op=ALU.mult),
                   reads=["YY1", "rz"], writes=["tU"])
                OP("dve", lambda e: e.reduce_sum(out=imp[:], in_=tU[:].rearrange("p r n -> p n r"), axis=AX.X),
                   reads=["tU"], writes=["imp"])
                lo = 128 - 2 * i
                OP("pool", lambda e: e.tensor_tensor(out=t2[:], in0=imp[:], in1=Tm[:, lo:lo + 128], op=ALU.mult),
                   reads=["imp", "Tm"], writes=["t2"])
                OP("pool", lambda e: e.tensor_tensor(out=t2[:], in0=t2[:], in1=Ta[:, lo:lo + 128], op=ALU.add),
                   reads=["t2", "Ta"], writes=["t2"])
                OP("pool", lambda e: e.memset(t2[:, 0:1], 3e30), writes=["t2"])
                OP("dve", lambda e: e.max(out=m1[:], in_=t2[:]), reads=["t2"], writes=["m1"])
                OP("dve", lambda e: e.match_replace(out=t3[:], in_to_replace=m1[:], in_values=t2[:], imm_value=-3e38),
                   reads=["t2", "m1"], writes=["t3"])
                OP("dve", lambda e: e.max(out=m2[:], in_=t3[:]), reads=["t3"], writes=["m2"])
                OP("dve", lambda e: e.tensor_scalar(out=mqb[:], in0=t2[:], scalar1=m2[:, 7:8], scalar2=NEG,
                                                    op0=ALU.is_lt, op1=ALU.mult), reads=["t2", "m2"], writes=[mqk])

            def maskbuild(i, g, sl):
                mqk = f"mq{sl}{g}"; mqb = mqs[sl][g]
                OP("pe", lambda e: e.transpose(TT[:], mqb[:], identb[:]), reads=[mqk, "identb"], writes=["TT"])
                OP("dve", lambda e: e.tensor_copy(out=MnR[sl][g][:], in_=TT[:].unsqueeze(1).to_broadcast([128, 4, 128])),
                   reads=["TT"], writes=[f"MnR{sl}{g}"])
                OP("dve", lambda e: e.tensor_tensor(out=MnT[sl][g][:], in0=TT[:].unsqueeze(1).to_broadcast([128, 4, 128]),
                                                    in1=c8b[:, 4 * g:4 * g + 4].unsqueeze(2).to_broadcast([128, 4, 128]),
                                                    op=ALU.add), reads=["TT", "c8b"], writes=[f"MnT{sl}{g}"])

            def stageA_load(i):
                sl = i % 2
                for g in range(2):
                    rs = slice(64 * g, 64 * g + 64)
                    sc.dma("sp", QTz[sl][g][rs, :, :], qT_s[rs, :, i * 128:(i + 1) * 128],
                           writes=[f"QTz{sl}{g}"], slot=f"QTz{sl}{g}")
                js = [j for j in range(NCT) if i - 16 * j >= 0]
                nearmap = {}
                for j in js:
                    m = i - 16 * j
                    if m <= 17:
                        k = len(nearmap)
                        nearmap[j] = k
                        sc.dma("sp", cmpB[sl][k][:], cmpBias[m], writes=[f"cmpB{sl}{k}"], slot=f"cmpB{sl}{k}")
                return js, nearmap

            def stageA_g(i, g, js, nearmap):
                sl = i % 2
                OC = YY[:, 0, 0:260]; U = YY[:, 1, :]
                first = True
                for j in js:
                    lhsT = kcT[:, j * 128:(j + 1) * 128]
                    if j in nearmap:
                        k = nearmap[j]
                        sargs = (lhsT, ["kcT"], sl, g, None, (cmpB[sl][k][:, 4 * g:4 * g + 4, :], f"cmpB{sl}{k}"))
                    else:
                        sargs = (lhsT, ["kcT"], sl, g,
                                 (onesrow[:], c8rhs[g][:].rearrange("p r q -> p (r q)"), ["onesrow", f"c8rhs{g}"]), None)
                    unit(sargs, [(OC, "YY0", 65, vcA[:, j, g, :], ["vcA"], first),
                                 (U, "YY1", 128, ovlb[:, j, :], ["ovlb"], first)])
                    first = False
                pend_pv.append([2, lambda: finalizeA(i, g, sl)])
                pend_late.append([32, lambda: maskbuild(i, g, sl), i])

            def finalizeB1(g, psx, key, dst, dk):
                v = psx[:, 0:260].rearrange("p (r d) -> p r d", r=4)
                OP("dve", lambda e: e.reciprocal(out=rz[:], in_=v[:, :, 64]), reads=[key], writes=["rz"])
                OP("dve", lambda e: e.tensor_tensor(
                    out=dst[:, 4 * g:4 * g + 4, :], in0=v[:, :, 0:64],
                    in1=rz[:].unsqueeze(2).to_broadcast([128, 4, 64]), op=ALU.mult), reads=[key, "rz"], writes=[dk])

            def combine(i, sl):
                tsl = slice(i * 128, (i + 1) * 128)
                gt = gts[sl]; znt = znts[sl]
                gv = gt[:].rearrange("p (h b) -> p h b", b=3)
                OP("pool", lambda e: e.tensor_tensor(out=acc[:], in0=ocn[sl][:], in1=gv[:, :, 0:1].to_broadcast([128, 8, 64]),
                                                     op=ALU.mult), reads=[f"ocn{sl}", f"gt{sl}"], writes=["acc"])
                for src, sk, b_ in ((osn, "osn", 1), (own, "own", 2)):
                    OP("pool", lambda e, src=src, b_=b_: e.tensor_tensor(out=tacc[:], in0=src[:],
                                                                         in1=gv[:, :, b_:b_ + 1].to_broadcast([128, 8, 64]), op=ALU.mult),
                       reads=[sk, f"gt{sl}"], writes=["tacc"])
                    OP("pool", lambda e: e.tensor_tensor(out=acc[:], in0=acc[:], in1=tacc[:], op=ALU.add),
                       reads=["acc", "tacc"], writes=["acc"])
                OP("pool", lambda e: e.tensor_tensor(out=mixn[:], in0=acc[:].rearrange("p h d -> p (h d)"), in1=znt[:], op=ALU.mult),
                   reads=["acc", f"znt{sl}"], writes=["mixn"])
                sc.dma("sp", mixn_s[tsl, :], mixn[:], reads=["mixn"], writes=["mixn_s"], slot="mixn", multi=True)

            def stageB_pre(i):
                sl = i % 2
                tsl = slice(i * 128, (i + 1) * 128)
                if pend_late and pend_late[0][2] == i:
                    while pend_pv:
                        pend_pv.pop(0)[1]()
                    while pend_late and pend_late[0][2] == i:
                        pend_late.pop(0)[1]()
                sc.dma("sp", gts[sl][:], gates_s[tsl, :], reads=["gates_s"], writes=[f"gt{sl}"], slot=f"gt{sl}")
                sc.dma("sp", znts[sl][:], zn_s[tsl, :], reads=["zn_s"], writes=[f"znt{sl}"], slot=f"znt{sl}")

            def stageB_g(i, g):
                sl = i % 2
                first = True
                for j in range(0, i + 1):
                    dd = i - j
                    lhsT = KsT[:, j * 128:(j + 1) * 128]
                    Fl = Fb[:, j * 128:(j + 1) * 128]
                    if dd >= 2:
                        sargs = (lhsT, [], sl, g, (Fl, MnT[sl][g][:].rearrange("p r q -> p (r q)"), [f"MnT{sl}{g}", "Fb"]), None)
                    else:
                        sargs = (lhsT, [], sl, g, [(Fl, MnR[sl][g][:].rearrange("p r q -> p (r q)"), [f"MnR{sl}{g}", "Fb"]),
                                                   (identb[:], bnH[:, dd, 4 * g:4 * g + 4, :], ["identb", "bnH"]),
                                                   (identb[:], bnL[:, dd, 4 * g:4 * g + 4, :], ["identb", "bnL"])], None)
                    unit(sargs, [(OSp, "OSp", 65, Vs[:, j, g, :], [], first)])
                    first = False
                pend_pv.append([2, lambda: finalizeB1(g, OSp, "OSp", osn, "osn")])
                first = True
                for j in range(max(0, i - 4), i + 1):
                    dd = i - j
                    lhsT = KwT[:, j * 128:(j + 1) * 128]
                    if dd == 4:
                        sargs = (lhsT, [], sl, g, (identb[:], B4[g][:].rearrange("p r q -> p (r q)"), [f"B4{g}", "identb"]), None)
                    elif dd >= 2:
                        sargs = (lhsT, [], sl, g,
                                 (onesrow[:], c8rhs[g][:].rearrange("p r q -> p (r q)"), ["onesrow", f"c8rhs{g}"]), None)
                    else:
                        sargs = (lhsT, [], sl, g, [(identb[:], bnH[:, dd, 4 * g:4 * g + 4, :], ["identb", "bnH"]),
                                                   (identb[:], bnL[:, dd, 4 * g:4 * g + 4, :], ["identb", "bnL"])], None)
                    unit(sargs, [(OWp, "OWp", 65, Vw[:, j, g, :], [], first)])
                    first = False
                pend_pv.append([2, lambda: finalizeB1(g, OWp, "OWp", own, "own")])

            info = stageA_load(0)
            for g in range(2):
                stageA_g(0, g, *info)
            for i in range(NT):
                if i + 1 < NT:
                    info = stageA_load(i + 1)
                stageB_pre(i)
                for g in range(2):
                    if i + 1 < NT:
                        stageA_g(i + 1, g, *info)
                    stageB_g(i, g)
                pend_pv.append([2, lambda i=i: combine(i, i % 2)])
            flush_all()
            sc.barrier()

        pre2.close()
        pers.close()
        with ExitStack() as p3:
            wst3 = sbt(p3, "wst3", [128, 1024], F32)
            woutb = sbt(p3, "woutb", [128, 8, 1024], BF16)
            wpgb = sbt(p3, "wpgb", [128, 8, 1024], BF16)
            wpeb = sbt(p3, "wpeb", [128, 2, 1024], BF16)
            k = 0
            for wsrc, wdst, nck, key in ((w_out, woutb, 8, "woutb"), (w_pg, wpgb, 8, "wpgb"), (w_pe, wpeb, 2, "wpeb")):
                for c in range(nck):
                    sc.dma("sp", wst3[:], wsrc[c * 128:(c + 1) * 128, :], writes=["wst3"], slot="wst3")
                    OP(("dve", "pool")[k % 2], lambda e, c=c, wdst=wdst: e.tensor_copy(out=wdst[:, c, :], in_=wst3[:]),
                       reads=["wst3"], writes=[key])
                    k += 1
            bt = {}
            for nm, src in (("g0B", ln0g_r), ("b0B", ln0b_r), ("lngB", lng_r), ("lnbB", lnb_r), ("bpgB", bpg_r)):
                bt[nm] = sbt(p3, nm, [128, 1024], F32)
                sc.dma("sp", bt[nm][:], src.partition_broadcast(128), writes=[nm], slot="c_" + nm)
            NS3 = 5
            ag0B = sbt(p3, "ag0B", [128, 1024], F32)
            ab0B = sbt(p3, "ab0B", [128, 1024], F32)
            OP("dve", lambda e: e.tensor_scalar(out=ag0B[:], in0=bt["g0B"][:], scalar1=ALPHA, scalar2=None, op0=ALU.mult),
               reads=["g0B"], writes=["ag0B"])
            OP("dve", lambda e: e.tensor_scalar(out=ab0B[:], in0=bt["b0B"][:], scalar1=ALPHA, scalar2=None, op0=ALU.mult),
               reads=["b0B"], writes=["ab0B"])
            T3s = [(pst(p3, f"T3_{k}", [128, 8, 128], BF16), f"T3_{k}") for k in range(2)]
            t3rot = [0]

            def mkset(k):
                d = {}
                for nm, shp, dt_ in (("x3", [128, 1024], F32), ("mn3", [128, 1024], BF16), ("pt3", [128, 256], F32),
                                     ("ptb", [128, 256], BF16), ("pT3", [128, 2, 128], BF16), ("mixT", [128, 8, 128], BF16),
                                     ("s123", [128, 4], F32), ("mv3", [128, 2], F32), ("rstd3", [128, 1], F32),
                                     ("nmr3", [128, 1], F32),
                                     ("hh", [128, 1024], F32), ("rr", [128, 1024], F32), ("rbf", [128, 1024], BF16),
                                     ("rT", [128, 8, 128], BF16), ("gg", [128, 1024], F32)):
                    d[nm] = (sbt(p3, f"{nm}_{k}", shp, dt_), f"{nm}_{k}")
                d["xo"] = d["hh"]
                d["Y3"] = (pst(p3, f"Y3_{k}", [128, 512], F32), f"Y3_{k}")
                return d

            sets3 = [mkset(k) for k in range(NS3)]

            def tile3(i):
                B_ = sets3[i % NS3]
                Y3, Y3k = B_["Y3"]
                xb, xk = B_["x3"]; mn3, mnk = B_["mn3"]; pt3, ptk = B_["pt3"]; ptb, ptbk = B_["ptb"]
                pT3, pTk = B_["pT3"]; mixT, mixTk = B_["mixT"]; s123, s123k = B_["s123"]; mv3, mv3k = B_["mv3"]
                rstd3, rstd3k = B_["rstd3"]; hh, hhk = B_["hh"]; rr, rrk = B_["rr"]; rbf, rbfk = B_["rbf"]
                rT, rTk = B_["rT"]; gg, ggk = B_["gg"]; xob, xok = B_["xo"]; nmr, nmrk = B_["nmr3"]

                def lnstats(src, skey):
                    OP("dve", lambda e: e.reduce_sum(out=s123[:, 0:1], in_=src[:], axis=AX.X), reads=[skey], writes=[s123k]); yield
                    OP("act", lambda e: e.activation(out=gg[:], in_=src[:], func=AF.Square, accum_out=s123[:, 1:2]),
                       reads=[skey], writes=[s123k + "b", ggk]); yield
                    OP("dve", lambda e: e.tensor_scalar(out=mv3[:, 0:1], in0=s123[:, 0:1], scalar1=1.0 / 1024.0, scalar2=None,
                                                        op0=ALU.mult), reads=[s123k], writes=[mv3k]); yield
                    OP("dve", lambda e: e.tensor_tensor(out=s123[:, 2:3], in0=mv3[:, 0:1], in1=mv3[:, 0:1], op=ALU.mult),
                       reads=[mv3k], writes=[s123k + "c"]); yield
                    OP("dve", lambda e: e.scalar_tensor_tensor(out=mv3[:, 1:2], in0=s123[:, 1:2], scalar=1.0 / 1024.0,
                                                               in1=s123[:, 2:3], op0=ALU.mult, op1=ALU.subtract),
                       reads=[s123k + "b", s123k + "c"], writes=[mv3k]); yield
                    OP("dve", lambda e: e.tensor_scalar(out=rstd3[:], in0=mv3[:, 1:2], scalar1=1.0, scalar2=EPS,
                                                        op0=ALU.mult, op1=ALU.add), reads=[mv3k], writes=[rstd3k]); yield
                    OP("act", lambda e: e.activation(out=rstd3[:], in_=rstd3[:], func=AF.Sqrt), reads=[rstd3k], writes=[rstd3k]); yield
                    OP("dve", lambda e: e.reciprocal(out=rstd3[:], in_=rstd3[:]), reads=[rstd3k], writes=[rstd3k]); yield
                    OP("dve", lambda e: e.scalar_tensor_tensor(out=nmr[:], in0=mv3[:, 0:1], scalar=-1.0, in1=rstd3[:],
                                                               op0=ALU.mult, op1=ALU.mult), reads=[mv3k, rstd3k], writes=[nmrk]); yield

                def transposes(src, skey, n, dst, dkey):
                    t3rot[0] = (t3rot[0] + 1) % 2
                    T3, T3k = T3s[t3rot[0]]
                    for c in range(n):
                        OP("pe", lambda e, c=c: e.transpose(T3[:, c, :], src[:, c * 128:(c + 1) * 128], identb[:]),
                           reads=[skey, "identb"], writes=[T3k])
                        if c % 4 == 3:
                            yield
                    OP("act", lambda e: e.copy(out=dst[:, 0:n, :], in_=T3[:, 0:n, :]), reads=[T3k], writes=[dkey]); yield

                def mmh(lhs, lkey, wb, wkey, nck, half):
                    for c in range(nck):
                        OP("pe", lambda e, c=c: e.matmul(Y3[:], lhsT=lhs[:, c, :],
                                                         rhs=wb[:, c, half * 512:(half + 1) * 512],
                                                         start=(c == 0), stop=(c == nck - 1)),
                           reads=[lkey, wkey], writes=[Y3k])
                        if c % 4 == 3:
                            yield
                    yield

                H = lambda ap, half: ap[:, half * 512:(half + 1) * 512]
                v2 = lambda ap: ap.rearrange("p (a b) -> p a b", a=2)
                tsl = slice(i * 128, (i + 1) * 128)
                sc.dma("sp", xb[:], x[tsl, :], writes=[xk], slot=xk)
                sc.dma("sp", mn3[:, 0:512], mixn_s[tsl, :], reads=["mixn_s"], writes=[mnk], slot=mnk)
                sc.dma("sp", mn3[:, 512:1024], mixg_s[tsl, :], reads=["mixg_s"], writes=[mnk], slot=mnk)
                sc.dma("sp", pt3[:], p_in[tsl, :], writes=[ptk], slot=ptk)
                yield
                yield from transposes(mn3, mnk, 8, mixT, mixTk)
                yield from lnstats(xb, xk)
                OP("act", lambda e: e.copy(out=ptb[:], in_=pt3[:]), reads=[ptk], writes=[ptbk]); yield
                OP("act", lambda e: e.activation(out=hh[:], in_=xb[:], func=AF.Identity, bias=nmr[:, 0:1], scale=rstd3[:, 0:1]),
                   reads=[xk, nmrk, rstd3k], writes=[hhk]); yield
                OP("dve", lambda e: e.tensor_tensor(out=hh[:], in0=hh[:], in1=ag0B[:], op=ALU.mult),
                   reads=[hhk, "ag0B"], writes=[hhk]); yield
                for half in range(2):
                    yield from mmh(mixT, mixTk, woutb, "woutb", 8, half)
                    OP("dve", lambda e, half=half: e.tensor_tensor(out=H(hh, half), in0=H(hh, half), in1=Y3[:], op=ALU.add),
                       reads=[hhk, Y3k], writes=[hhk]); yield
                OP("pool", lambda e: e.tensor_tensor(out=rr[:], in0=hh[:], in1=ab0B[:], op=ALU.add),
                   reads=[hhk, "ab0B"], writes=[rrk]); yield
                OP("act", lambda e: e.copy(out=rbf[:], in_=rr[:]), reads=[rrk], writes=[rbfk]); yield
                yield from transposes(rbf, rbfk, 8, rT, rTk)
                for half in range(2):
                    yield from mmh(rT, rTk, wpgb, "wpgb", 8, half)
                    OP("dve", lambda e, half=half: e.tensor_tensor(out=H(gg, half), in0=Y3[:], in1=H(bt["bpgB"], half), op=ALU.add),
                       reads=[Y3k, "bpgB"], writes=[ggk]); yield
                OP("act", lambda e: e.activation(out=gg[:], in_=gg[:], func=AF.Sigmoid), reads=[ggk], writes=[ggk]); yield
                yield from transposes(ptb, ptbk, 2, pT3, pTk)
                for half in range(2):
                    yield from mmh(pT3, pTk, wpeb, "wpeb", 2, half)
                    OP("dve", lambda e, half=half: e.tensor_tensor(out=H(gg, half), in0=H(gg, half), in1=Y3[:], op=ALU.mult),
                       reads=[ggk, Y3k], writes=[ggk]); yield
                OP("pool", lambda e: e.tensor_tensor(out=rr[:], in0=rr[:], in1=gg[:], op=ALU.add), reads=[rrk, ggk], writes=[rrk]); yield
                yield from lnstats(rr, rrk)
                OP("act", lambda e: e.activation(out=xob[:], in_=rr[:], func=AF.Identity, bias=nmr[:, 0:1], scale=rstd3[:, 0:1]),
                   reads=[rrk, nmrk, rstd3k], writes=[xok]); yield
                OP("dve", lambda e: e.tensor_tensor(out=xob[:], in0=xob[:], in1=bt["lngB"][:], op=ALU.mult),
                   reads=[xok, "lngB"], writes=[xok]); yield
                OP("pool", lambda e: e.tensor_tensor(out=xob[:], in0=xob[:], in1=bt["lnbB"][:], op=ALU.add),
                   reads=[xok, "lnbB"], writes=[xok]); yield
                sc.dma("sp", out[tsl, :], xob[:], reads=[xok], writes=["out"], slot=xok, multi=True)
                yield

            run_interleaved((tile3(i) for i in range(NT)), NS3, stagger=13)
            sc.finish("sp")
            sc.finish("pool")
    return nc


_CACHE = {}
DBG = False
LAST = None


def make_inputs(S, b, inp, consts):
    f = lambda a: np.ascontiguousarray(a, dtype=np.float32)
    m = {
        "x": f(inp["x"][b]), "p": f(inp["p"][0, b]), "w_in": f(inp["w_in"][0][:, PERM]),
        "ln0gT": f(inp["ln0_g"].reshape(8, 128).T), "ln0bT": f(inp["ln0_b"].reshape(8, 128).T),
        "ln0g_r": f(inp["ln0_g"].reshape(1, D)), "ln0b_r": f(inp["ln0_b"].reshape(1, D)),
        "lng_r": f(inp["ln_g"][0].reshape(1, D)), "lnb_r": f(inp["ln_b"][0].reshape(1, D)),
        "bpg_r": f(inp["b_pg"][0].reshape(1, D)), "rb31_r": f(inp["rel_bias"][31].reshape(1, 8)),
        "w_a2": f(inp["w_a2"][0]), "baT": f(inp["b_a"][0].reshape(2, 128).T),
        "gnw_r": f(inp["gla_norm_w"][0].reshape(1, 128)),
        "posT": f(np.concatenate([inp["pos_cmp"][0].T, inp["pos_cmp"][0].T], axis=0)),
        "w_ck1": f(inp["w_ck1"][0]), "w_cv1": f(inp["w_cv1"][0]),
        "b_ck1T": f(inp["b_ck1"][0].reshape(2, 128).T), "b_cv1T": f(inp["b_cv1"][0].reshape(2, 128).T),
        "w_ck2": f(inp["w_ck2"][0]), "w_cv2": f(inp["w_cv2"][0]),
        "w_out": f(inp["w_out"][0]), "w_pe": f(inp["w_pe"][0]), "w_pg": f(inp["w_pg"][0]),
    }
    m.update(consts)
    return m


def kernel(**inputs):
    inp = {k: np.asarray(v) for k, v in inputs.items()}
    B, S, _ = inp["x"].shape
    if S not in _CACHE:
        _CACHE[S] = build(S, dbg=DBG)
    nc = _CACHE[S]
    consts = {k: np.ascontiguousarray(v, dtype=np.float32) for k, v in host_consts(S, inp["rel_bias"].astype(np.float32)).items()}
    in_maps = [make_inputs(S, b, inp, consts) for b in range(B)]
    res = run_bass_kernel_spmd(nc, in_maps, core_ids=list(range(B)))
    if DBG:
        global LAST
        LAST = res.results
    return np.stack([np.asarray(r["out"]) for r in res.results], axis=0).astype(np.float32)
```
